# Optimizing a Trainium2 kernel written in Bass

```python
import math
import jax, jax.numpy as jnp
from jax import lax
import numpy as np

D_MODEL = 1024
BATCH = 8
SEQ = 2048
DEPTH = 2

ROPE_THETA = 500000.0
NEG_INF = -1e30
LN_EPS = 1e-5
RMS_EPS = 1e-6
DEEPNORM_ALPHA = (2 * DEPTH) ** 0.25
DEEPNORM_BETA = (8 * DEPTH) ** -0.25
POS_OFFSET_MAX = 4096
MLA_HEADS = 8
MLA_NOPE = 64
MLA_ROPE = 32
MLA_V = 64
MLA_Q_RANK = 768
MLA_KV_RANK = 256
ATTN_Q_BLOCK = 128
MOBA_HEADS = 8
MOBA_HEAD_DIM = 64
MOBA_ROT_DIM = MOBA_HEAD_DIM // 4
MOBA_BLOCK = 256
MOBA_TOPK = 3
MOBA_Q_CHUNK = 16
GDN_HEADS = 8
GDN_HEAD_DIM = 64
GDN_CONV = 4
GDN_CHUNK = 64
N_BRANCH = 3
BRANCH_WIDTH = 512
MLA_WIDTH = MLA_HEADS * MLA_V
MOBA_WIDTH = MOBA_HEADS * MOBA_HEAD_DIM
GDN_WIDTH = GDN_HEADS * GDN_HEAD_DIM
IN_SPLITS = (MLA_Q_RANK, MLA_KV_RANK, MLA_ROPE, 3 * MOBA_WIDTH, 3 * GDN_WIDTH, GDN_WIDTH, GDN_HEADS, GDN_HEADS, N_BRANCH * D_MODEL)
IN_COLS = sum(IN_SPLITS)
PEER_HEADS = 8
N_KEYS = 128
N_EXPERTS = N_KEYS * N_KEYS
PEER_TOPK = 16
PEER_QDIM = 256
PEER_HALF = PEER_QDIM // 2
PEER_TOKEN_CHUNK = 128

kernel_name = 'hybrid_mla_moba_gdn_peer_deepnorm'


def layer_norm(x, g, b):
    xf = x.astype(jnp.float32)
    mu = jnp.mean(xf, axis=-1, keepdims=True)
    var = jnp.mean(jnp.square(xf - mu), axis=-1, keepdims=True)
    return ((xf - mu) * lax.rsqrt(var + LN_EPS) * g + b).astype(x.dtype)


def rms_norm(x, g):
    xf = x.astype(jnp.float32)
    return (xf * lax.rsqrt(jnp.mean(xf * xf, axis=-1, keepdims=True) + RMS_EPS) * g).astype(x.dtype)


def l2_normalize(t):
    tf = t.astype(jnp.float32)
    return tf * lax.rsqrt(jnp.sum(tf * tf, axis=-1, keepdims=True) + RMS_EPS)


def rope_tables(positions, rot_dim):
    inv_freq = ROPE_THETA ** (-jnp.arange(0, rot_dim, 2, dtype=jnp.float32) / rot_dim)
    ang = positions.astype(jnp.float32)[..., None] * inv_freq
    return jnp.cos(ang), jnp.sin(ang)


def apply_rope(x, cos, sin):
    r = cos.shape[-1]
    c = cos[:, :, None, :].astype(x.dtype)
    s = sin[:, :, None, :].astype(x.dtype)
    x1, x2, rest = x[..., :r], x[..., r:2 * r], x[..., 2 * r:]
    return jnp.concatenate([x1 * c - x2 * s, x2 * c + x1 * s, rest], axis=-1)


def causal_attention_blocked(q, k, v, scale):
    B, S, H, dk = q.shape
    nq = S // ATTN_Q_BLOCK
    qb = q.reshape(B, nq, ATTN_Q_BLOCK, H, dk).transpose(1, 0, 2, 3, 4)
    kpos = jnp.arange(S)

    def one_block(args):
        i, qi = args
        s = jnp.einsum('bqhd,bkhd->bhqk', qi, k, preferred_element_type=jnp.float32) * scale
        qpos = i * ATTN_Q_BLOCK + jnp.arange(ATTN_Q_BLOCK)
        s = jnp.where(kpos[None, :] <= qpos[:, None], s, NEG_INF)
        p = jax.nn.softmax(s, axis=-1).astype(v.dtype)
        return jnp.einsum('bhqk,bkhd->bqhd', p, v)

    out = lax.map(one_block, (jnp.arange(nq), qb))
    return out.transpose(1, 0, 2, 3, 4).reshape(B, S, H, v.shape[-1])


def mla_branch(c_q, c_kv, k_rope_in, q_norm_g, kv_norm_g, w_uq, w_ukv, cos, sin):
    B, S, _ = c_q.shape
    q = (rms_norm(c_q, q_norm_g) @ w_uq).reshape(B, S, MLA_HEADS, MLA_NOPE + MLA_ROPE)
    q = jnp.concatenate([q[..., :MLA_NOPE], apply_rope(q[..., MLA_NOPE:], cos, sin)], axis=-1)
    kv = (rms_norm(c_kv, kv_norm_g) @ w_ukv).reshape(B, S, MLA_HEADS, MLA_NOPE + MLA_V)
    k_nope, v = kv[..., :MLA_NOPE], kv[..., MLA_NOPE:]
    k_rope = apply_rope(k_rope_in[:, :, None, :], cos, sin)
    k = jnp.concatenate([k_nope, jnp.broadcast_to(k_rope, (B, S, MLA_HEADS, MLA_ROPE))], axis=-1)
    o = causal_attention_blocked(q, k, v, (MLA_NOPE + MLA_ROPE) ** -0.5)
    return o.reshape(B, S, MLA_WIDTH)


def moba_branch(q, k, v, cos, sin):
    B, S, H, dh = q.shape
    q = apply_rope(q, cos, sin)
    k = apply_rope(k, cos, sin)
    nb = -(-S // MOBA_BLOCK)
    n_sel = min(MOBA_TOPK, nb)
    pad = nb * MOBA_BLOCK - S
    kp = jnp.pad(k, ((0, 0), (0, pad), (0, 0), (0, 0)))
    vp = jnp.pad(v, ((0, 0), (0, pad), (0, 0), (0, 0)))
    k_blk = kp.reshape(B, nb, MOBA_BLOCK, H, dh).transpose(0, 3, 1, 2, 4)
    v_blk = vp.reshape(B, nb, MOBA_BLOCK, H, dh).transpose(0, 3, 1, 2, 4)
    k_mean = jnp.mean(k_blk.astype(jnp.float32), axis=3)
    scale = dh ** -0.5
    nqc = S // MOBA_Q_CHUNK
    q_chunks = q.reshape(B, nqc, MOBA_Q_CHUNK, H, dh).transpose(1, 0, 3, 2, 4)
    b_idx = jnp.arange(B)[:, None, None, None]
    h_idx = jnp.arange(H)[None, :, None, None]
    blk_ids = jnp.arange(nb)

    def one_chunk(args):
        c, qc = args
        q0 = c * MOBA_Q_CHUNK
        own = q0 // MOBA_BLOCK
        gate = jnp.einsum('bhqd,bhnd->bhqn', qc.astype(jnp.float32), k_mean)
        gate = jnp.where(blk_ids < own, gate, NEG_INF)
        _, sel = lax.top_k(gate, n_sel)
        valid = sel < own
        kg = k_blk[b_idx, h_idx, sel]
        vg = v_blk[b_idx, h_idx, sel]
        s_past = jnp.einsum('bhqd,bhqjpd->bhqjp', qc, kg, preferred_element_type=jnp.float32) * scale
        s_past = jnp.where(valid[..., None], s_past, NEG_INF).reshape(B, H, MOBA_Q_CHUNK, n_sel * MOBA_BLOCK)
        k_own = lax.dynamic_index_in_dim(k_blk, own, axis=2, keepdims=False)
        v_own = lax.dynamic_index_in_dim(v_blk, own, axis=2, keepdims=False)
        s_own = jnp.einsum('bhqd,bhpd->bhqp', qc, k_own, preferred_element_type=jnp.float32) * scale
        qpos = q0 + jnp.arange(MOBA_Q_CHUNK)
        kpos = own * MOBA_BLOCK + jnp.arange(MOBA_BLOCK)
        s_own = jnp.where(kpos[None, :] <= qpos[:, None], s_own, NEG_INF)
        p = jax.nn.softmax(jnp.concatenate([s_past, s_own], axis=-1), axis=-1).astype(v.dtype)
        p_past = p[..., :n_sel * MOBA_BLOCK].reshape(B, H, MOBA_Q_CHUNK, n_sel, MOBA_BLOCK)
        p_own = p[..., n_sel * MOBA_BLOCK:]
        return jnp.einsum('bhqjp,bhqjpd->bhqd', p_past, vg) + jnp.einsum('bhqp,bhpd->bhqd', p_own, v_own)

    out = lax.map(one_chunk, (jnp.arange(nqc), q_chunks))
    return out.transpose(1, 0, 3, 2, 4).reshape(B, S, MOBA_WIDTH)


def causal_depthwise_conv(x, w):
    K = w.shape[0]
    return lax.conv_general_dilated(x, w[:, None, :].astype(x.dtype), window_strides=(1,), padding=((K - 1, 0),),
                                    dimension_numbers=('NWC', 'WIO', 'NWC'), feature_group_count=x.shape[-1])


def chunk_gated_delta_rule(q, k, v, g, beta):
    B, S, H, dk = q.shape
    dv = v.shape[-1]
    C = GDN_CHUNK
    n = S // C
    f32 = jnp.float32

    def to_chunks(t):
        return t.astype(f32).reshape(B, n, C, H, -1).transpose(0, 3, 1, 2, 4)

    q, k, v = to_chunks(q), to_chunks(k), to_chunks(v)
    g = g.astype(f32).reshape(B, n, C, H).transpose(0, 3, 1, 2)
    beta = beta.astype(f32).reshape(B, n, C, H).transpose(0, 3, 1, 2)
    g_cum = jnp.cumsum(g, axis=-1)
    tril = jnp.tril(jnp.ones((C, C), dtype=bool))
    strict = jnp.tril(jnp.ones((C, C), dtype=bool), -1)
    decay = jnp.exp(jnp.where(tril, g_cum[..., :, None] - g_cum[..., None, :], -jnp.inf))
    k_beta = k * beta[..., None]
    A = jnp.where(strict, jnp.einsum('bhncd,bhnsd->bhncs', k_beta, k) * decay, 0.0)
    lhs = jnp.eye(C, dtype=f32) + A
    rhs = jnp.concatenate([v * beta[..., None], k_beta * jnp.exp(g_cum)[..., None]], axis=-1)
    sol = lax.linalg.triangular_solve(lhs, rhs, left_side=True, lower=True, unit_diagonal=True)
    u, w = sol[..., :dv], sol[..., dv:]
    qk = jnp.where(tril, jnp.einsum('bhncd,bhnsd->bhncs', q, k) * decay, 0.0)

    def step(state, xs):
        q_c, k_c, u_c, w_c, g_c, qk_c = xs
        v_new = u_c - jnp.einsum('bhcd,bhde->bhce', w_c, state)
        o_c = jnp.einsum('bhcd,bhde->bhce', q_c * jnp.exp(g_c)[..., None], state) + jnp.einsum('bhcs,bhse->bhce', qk_c, v_new)
        g_last = g_c[..., -1]
        k_dec = k_c * jnp.exp(g_last[..., None] - g_c)[..., None]
        state = state * jnp.exp(g_last)[..., None, None] + jnp.einsum('bhcd,bhce->bhde', k_dec, v_new)
        return state, o_c

    xs = tuple(jnp.moveaxis(t, 2, 0) for t in (q, k, u, w, g_cum, qk))
    _, o = lax.scan(step, jnp.zeros((B, H, dk, dv), f32), xs)
    return o.transpose(1, 0, 3, 2, 4).reshape(B, S, H, dv)


def gdn_branch(qkv, z, a, b_logit, conv_w, A_log, dt_bias, o_norm_g):
    B, S, _ = qkv.shape
    qkv = jax.nn.silu(causal_depthwise_conv(qkv, conv_w))
    q, k, v = jnp.split(qkv, 3, axis=-1)
    q = l2_normalize(q.reshape(B, S, GDN_HEADS, GDN_HEAD_DIM)) * (GDN_HEAD_DIM ** -0.5)
    k = l2_normalize(k.reshape(B, S, GDN_HEADS, GDN_HEAD_DIM))
    v = v.reshape(B, S, GDN_HEADS, GDN_HEAD_DIM)
    beta = jax.nn.sigmoid(b_logit.astype(jnp.float32))
    g = -jnp.exp(A_log.astype(jnp.float32)) * jax.nn.softplus(a.astype(jnp.float32) + dt_bias)
    o = chunk_gated_delta_rule(q, k, v, g, beta)
    o = rms_norm(o, o_norm_g) * jax.nn.silu(z.reshape(B, S, GDN_HEADS, GDN_HEAD_DIM).astype(jnp.float32))
    return o.reshape(B, S, GDN_WIDTH).astype(z.dtype)


def token_mixer(h, cos_a, sin_a, cos_b, sin_b, w_in, mla_q_norm, mla_kv_norm, mla_w_uq, mla_w_ukv,
                gdn_conv_w, gdn_A_log, gdn_dt_bias, gdn_o_norm, gate_bias, w_branch, w_out):
    B, S, D = h.shape
    offs = np.cumsum(IN_SPLITS)[:-1].tolist()
    c_q, c_kv, k_rope, moba_qkv, gdn_qkv, gdn_z, gdn_a, gdn_b, gate_logits = jnp.split(h @ w_in, offs, axis=-1)
    o_a = mla_branch(c_q, c_kv, k_rope, mla_q_norm, mla_kv_norm, mla_w_uq, mla_w_ukv, cos_a, sin_a)
    mq, mk, mv = [t.reshape(B, S, MOBA_HEADS, MOBA_HEAD_DIM) for t in jnp.split(moba_qkv, 3, axis=-1)]
    o_b = moba_branch(mq, mk, mv, cos_b, sin_b)
    o_c = gdn_branch(gdn_qkv, gdn_z, gdn_a, gdn_b, gdn_conv_w, gdn_A_log, gdn_dt_bias, gdn_o_norm)
    branches = jnp.stack([o_a, o_b, o_c], axis=2)
    gates = jax.nn.sigmoid(gate_logits.reshape(B, S, N_BRANCH, D) + gate_bias)
    merged = jnp.sum(gates * jnp.einsum('bsnw,nwd->bsnd', branches, w_branch), axis=2)
    return merged @ w_out


def peer_ffn(h, w_q, sub_keys, expert_u, expert_v):
    B, S, D = h.shape
    T = B * S
    xt = h.reshape(T, D)
    qry = (xt @ w_q).reshape(T, PEER_HEADS, 2, PEER_HALF)
    sc = jnp.einsum('thpd,hpnd->thpn', qry, sub_keys, preferred_element_type=jnp.float32)
    s1, i1 = lax.top_k(sc[:, :, 0], PEER_TOPK)
    s2, i2 = lax.top_k(sc[:, :, 1], PEER_TOPK)
    cand_s = (s1[..., :, None] + s2[..., None, :]).reshape(T, PEER_HEADS, PEER_TOPK * PEER_TOPK)
    cand_i = (i1[..., :, None] * N_KEYS + i2[..., None, :]).reshape(T, PEER_HEADS, PEER_TOPK * PEER_TOPK)
    top_s, pos = lax.top_k(cand_s, PEER_TOPK)
    idx = jnp.take_along_axis(cand_i, pos, axis=-1)
    gate = jax.nn.softmax(top_s, axis=-1)
    n_chunks = T // PEER_TOKEN_CHUNK

    def one_chunk(args):
        xc, ic, gc = args
        u = expert_u[ic]
        act = jax.nn.gelu(jnp.einsum('td,thkd->thk', xc, u, preferred_element_type=jnp.float32), approximate=False)
        return jnp.einsum('thk,thkd->td', (gc * act).astype(h.dtype), expert_v[ic])

    out = lax.map(one_chunk, (xt.reshape(n_chunks, PEER_TOKEN_CHUNK, D),
                              idx.reshape(n_chunks, PEER_TOKEN_CHUNK, PEER_HEADS, PEER_TOPK),
                              gate.reshape(n_chunks, PEER_TOKEN_CHUNK, PEER_HEADS, PEER_TOPK)))
    return out.reshape(B, S, D)


def setup_inputs(seed: int = 0) -> dict:
    key = jax.random.key(seed)
    ks = jax.random.split(key, 32)
    f32 = jnp.float32
    L, D = DEPTH, D_MODEL

    def nrm(k, shape, scale):
        return jax.random.normal(k, shape, f32) * scale

    def gain(k, shape):
        return 1.0 + 0.05 * jax.random.normal(k, shape, f32)

    x = nrm(ks[0], (BATCH, SEQ, D), 1.0)
    positions = jax.random.randint(ks[1], (BATCH, 1), 0, POS_OFFSET_MAX, dtype=jnp.int32) + jnp.arange(SEQ, dtype=jnp.int32)[None, :]
    dt = jnp.exp(jax.random.uniform(ks[11], (L, GDN_HEADS), f32, math.log(1e-3), math.log(1e-1)))
    return {
        'x': x,
        'positions': positions,
        'ln_in_g': gain(ks[2], (D,)),
        'ln_in_b': nrm(ks[3], (D,), 0.02),
        'w_in': nrm(ks[4], (L, D, IN_COLS), D ** -0.5),
        'mla_q_norm': gain(ks[5], (L, MLA_Q_RANK)),
        'mla_kv_norm': gain(ks[6], (L, MLA_KV_RANK)),
        'mla_w_uq': nrm(ks[7], (L, MLA_Q_RANK, MLA_HEADS * (MLA_NOPE + MLA_ROPE)), MLA_Q_RANK ** -0.5),
        'mla_w_ukv': nrm(ks[8], (L, MLA_KV_RANK, MLA_HEADS * (MLA_NOPE + MLA_V)), MLA_KV_RANK ** -0.5),
        'gdn_conv_w': nrm(ks[9], (L, GDN_CONV, 3 * GDN_WIDTH), GDN_CONV ** -0.5),
        'gdn_A_log': jnp.log(jax.random.uniform(ks[10], (L, GDN_HEADS), f32, 1.0, 16.0)),
        'gdn_dt_bias': dt + jnp.log(-jnp.expm1(-dt)),
        'gdn_o_norm': gain(ks[12], (L, GDN_HEAD_DIM)),
        'gate_bias': nrm(ks[13], (L, N_BRANCH, D), 0.02),
        'w_branch': nrm(ks[14], (L, N_BRANCH, BRANCH_WIDTH, D), DEEPNORM_BETA * BRANCH_WIDTH ** -0.5),
        'w_out': nrm(ks[15], (L, D, D), DEEPNORM_BETA * D ** -0.5),
        'ln1_g': gain(ks[16], (L, D)),
        'ln1_b': nrm(ks[17], (L, D), 0.02),
        'peer_w_q': nrm(ks[18], (L, D, PEER_HEADS * PEER_QDIM), D ** -0.5),
        'peer_sub_keys': nrm(ks[19], (L, PEER_HEADS, 2, N_KEYS, PEER_HALF), PEER_HALF ** -0.5),
        'peer_u': nrm(ks[20], (L, N_EXPERTS, D), D ** -0.5),
        'peer_v': nrm(ks[21], (L, N_EXPERTS, D), DEEPNORM_BETA * PEER_HEADS ** -0.5),
        'ln2_g': gain(ks[22], (L, D)),
        'ln2_b': nrm(ks[23], (L, D), 0.02),
    }


def reference(x, positions, ln_in_g, ln_in_b, w_in, mla_q_norm, mla_kv_norm, mla_w_uq, mla_w_ukv,
              gdn_conv_w, gdn_A_log, gdn_dt_bias, gdn_o_norm, gate_bias, w_branch, w_out, ln1_g, ln1_b,
              peer_w_q, peer_sub_keys, peer_u, peer_v, ln2_g, ln2_b):
    cos_a, sin_a = rope_tables(positions, MLA_ROPE)
    cos_b, sin_b = rope_tables(positions, MOBA_ROT_DIM)
    h = layer_norm(x, ln_in_g, ln_in_b)
    for l in range(DEPTH):
        mix = token_mixer(h, cos_a, sin_a, cos_b, sin_b, w_in[l], mla_q_norm[l], mla_kv_norm[l], mla_w_uq[l],
                          mla_w_ukv[l], gdn_conv_w[l], gdn_A_log[l], gdn_dt_bias[l], gdn_o_norm[l],
                          gate_bias[l], w_branch[l], w_out[l])
        h = layer_norm(DEEPNORM_ALPHA * h + mix, ln1_g[l], ln1_b[l])
        ffn = peer_ffn(h, peer_w_q[l], peer_sub_keys[l], peer_u[l], peer_v[l])
        h = layer_norm(DEEPNORM_ALPHA * h + ffn, ln2_g[l], ln2_b[l])
    return h
```

```python
import math
from contextlib import ExitStack

import numpy as np
import concourse.bass as bass
import concourse.mybir as mybir
from concourse.bass_utils import run_bass_kernel_spmd

F32 = mybir.dt.float32
F32R = mybir.dt.float32r
I32 = mybir.dt.int32
U32 = mybir.dt.uint32
ALU = mybir.AluOpType
AF = mybir.ActivationFunctionType
AX = mybir.AxisListType

D = 1024
S = 2048
NT = S // 128
DEPTH = 2
IN_SPLITS = (768, 256, 32, 1536, 1536, 512, 8, 8, 3072)
IN_COLS = sum(IN_SPLITS)
OFF = np.cumsum((0,) + IN_SPLITS).tolist()
O_CQ, O_CKV, O_KR, O_MOBA, O_GQKV, O_GZ, O_GA, O_GB, O_GATE = OFF[:9]
ALPHA = (2 * DEPTH) ** 0.25
NEG = -30000.0
ENGS = ("pe", "dve", "act", "pool", "sp")
GDN_CHUNKS = 32
NT_LIMIT = 16
GDN_CUT = 0


class T:
    def __init__(self, ap, key):
        self.ap, self.key = ap, key

    def __getitem__(self, idx):
        return T(self.ap[idx], self.key)

    def k(self, sub):
        base = self.key[0] if isinstance(self.key, tuple) else self.key
        return T(self.ap, (base, sub))

    def re(self, pat, **kw):
        return T(self.ap.rearrange(pat, **kw), self.key)

    def bc(self, shape):
        return T(self.ap.broadcast_to(list(shape)), self.key)

    def un(self, axis):
        return T(self.ap.unsqueeze(axis), self.key)

    def cast(self, dt):
        return T(self.ap.bitcast(dt), self.key)

    @property
    def r(self):
        return T(self.ap.bitcast(F32R), self.key)


def _keys(ts):
    out = []
    for t in ts:
        if isinstance(t, T):
            out.append(t.key)
    return out


class Sched:
    def __init__(self, nc, es, n_dsem=44):
        self.nc = nc
        self.sem = {e: es.enter_context(nc.semaphore("sem_" + e)) for e in ENGS}
        self.dsem = [es.enter_context(nc.semaphore(f"dsem{i}")) for i in range(n_dsem)]
        self.dtgt = [0] * n_dsem
        self.dnext = 0
        self.dnext_sw = 0
        self.n_hw = n_dsem - 12
        self.ins = {e: [] for e in ENGS}
        self.lastc = {e: None for e in ENGS}
        self.lastw = {}
        self.readers = {}
        self.known = {e: {} for e in ENGS}

    def _deps(self, reads, writes):
        deps = set()
        for k in reads:
            t = self.lastw.get(k)
            if t is not None:
                deps.add(t)
        for k in writes:
            t = self.lastw.get(k)
            if t is not None:
                deps.add(t)
            for r in self.readers.get(k, ()):
                deps.add(r)
        return deps

    def _filter(self, eng, deps):
        waits = []
        known = self.known[eng]
        for t in deps:
            if t[0] == "e":
                if t[1] == "pe" and eng == "pe":
                    continue
                key, val = ("e", t[1]), t[2]
            else:
                key, val = ("d", t[1]), t[2]
            if known.get(key, -1) >= val:
                continue
            known[key] = val
            waits.append(t)
            if t[0] == "e":
                self.ins[t[1]][t[2]]["inc"] = True
        return waits

    def _commit(self, tok, reads, writes):
        for k in writes:
            self.lastw[k] = tok
            self.readers[k] = []
        for k in reads:
            if k not in writes:
                self.readers.setdefault(k, []).append(tok)

    def op(self, eng, fn, reads, writes):
        reads, writes = _keys(reads), _keys(writes)
        waits = self._filter(eng, self._deps(reads, writes))
        idx = len(self.ins[eng])
        self.ins[eng].append(dict(fn=fn, waits=waits, inc=False, dma=None))
        self.lastc[eng] = idx
        self._commit(("e", eng, idx), reads, writes)

    def dma(self, out, in_, eng="sp", fn=None, extra_reads=()):
        reads, writes = _keys([in_, *extra_reads]), _keys([out])
        if eng == "pool":
            s = self.n_hw + self.dnext_sw
            self.dnext_sw = (self.dnext_sw + 1) % (len(self.dsem) - self.n_hw)
        else:
            s = self.dnext
            self.dnext = (self.dnext + 1) % self.n_hw
        deps = self._deps(reads, writes)
        if self.dtgt[s] > 0:
            deps.add(("d", s, self.dtgt[s]))
        waits = self._filter(eng, deps)
        self.dtgt[s] += 16
        if fn is None:
            o, i = out.ap, in_.ap
            fn = lambda e: e.dma_start(out=o, in_=i)
        self.ins[eng].append(dict(fn=fn, waits=waits, inc=False, dma=s))
        self._commit(("d", s, self.dtgt[s]), reads, writes)

    def barrier(self):
        for e in ENGS:
            deps = set()
            for e2 in ENGS:
                if e2 != e and self.lastc[e2] is not None:
                    deps.add(("e", e2, self.lastc[e2]))
            for s, t in enumerate(self.dtgt):
                if t > 0:
                    deps.add(("d", s, t))
            waits = self._filter(e, deps)
            self.ins[e].append(dict(fn=None, waits=waits, inc=False, dma=None))
        self.lastw.clear()
        self.readers.clear()

    def emit(self):
        for e in ENGS:
            v = 0
            for i in self.ins[e]:
                if i["inc"]:
                    v += 1
                i["val"] = v
        sched = self

        def mk(e):
            def body(eng):
                if e == "pool":
                    sched.bnd_reg = eng.alloc_register("bnd")
                    eng.reg_mov(sched.bnd_reg, DEPTH * 16384 - 1)
                for i in sched.ins[e]:
                    for t in i["waits"]:
                        if t[0] == "e":
                            eng.wait_ge(sched.sem[t[1]], sched.ins[t[1]][t[2]]["val"])
                        else:
                            eng.wait_ge(sched.dsem[t[1]], t[2])
                    if i["fn"] is None:
                        continue
                    r = i["fn"](eng)
                    if i["dma"] is not None:
                        r.then_inc(sched.dsem[i["dma"]], 16)
                    elif i["inc"]:
                        r.then_inc(sched.sem[e], 1)
            return body

        with self.nc.Block() as block:
            block.tensor(mk("pe"))
            block.vector(mk("dve"))
            block.scalar(mk("act"))
            block.gpsimd(mk("pool"))
            block.sync(mk("sp"))

    def mm(self, out, lhsT, rhs, start=True, stop=True):
        o, a, b = out.ap, lhsT.ap, rhs.ap
        self.op("pe", lambda e: e.matmul(o, a, b, start=start, stop=stop), [lhsT, rhs], [out])

    def tr(self, out, in_, ident):
        o, a, b = out.ap, in_.ap, ident.ap
        self.op("pe", lambda e: e.transpose(o, a, b), [in_, ident], [out])

    def act(self, out, in_, func, bias=None, scale=1.0):
        o, a = out.ap, in_.ap
        kw = {}
        rd = [in_]
        if bias is not None:
            if isinstance(bias, T):
                kw["bias"] = bias.ap
                rd.append(bias)
            else:
                kw["bias"] = bias
        if isinstance(scale, T):
            rd.append(scale)
            sc = scale.ap
        else:
            sc = scale
        self.op("act", lambda e: e.activation(o, a, func, scale=sc, **kw), rd, [out])

    def tt(self, out, a, b, op, eng="dve"):
        o, x, y = out.ap, a.ap, b.ap
        self.op(eng, lambda e: e.tensor_tensor(o, x, y, op), [a, b], [out])

    def ts(self, out, a, s1, op0, s2=None, op1=None, eng="dve"):
        o, x = out.ap, a.ap
        rd = [a]
        v1 = s1
        if isinstance(s1, T):
            rd.append(s1)
            v1 = s1.ap
        v2 = s2
        if isinstance(s2, T):
            rd.append(s2)
            v2 = s2.ap
        if op1 is None:
            self.op(eng, lambda e: e.tensor_scalar(o, x, v1, None, op0), rd, [out])
        else:
            self.op(eng, lambda e: e.tensor_scalar(o, x, v1, v2, op0, op1), rd, [out])

    def stt(self, out, a, s, b, op0, op1, eng="dve"):
        o, x, y = out.ap, a.ap, b.ap
        rd = [a, b]
        v = s
        if isinstance(s, T):
            rd.append(s)
            v = s.ap
        self.op(eng, lambda e: e.scalar_tensor_tensor(o, x, v, y, op0, op1), rd, [out])

    def copy(self, out, in_, eng="dve"):
        o, a = out.ap, in_.ap
        if eng == "act":
            self.op("act", lambda e: e.copy(o, a), [in_], [out])
        else:
            self.op(eng, lambda e: e.tensor_copy(o, a), [in_], [out])

    def red(self, out, in_, op=ALU.add, axis=AX.X, eng="dve"):
        o, a = out.ap, in_.ap
        self.op(eng, lambda e: e.tensor_reduce(o, a, axis, op), [in_], [out])

    def recip(self, out, in_):
        o, a = out.ap, in_.ap
        self.op("dve", lambda e: e.reciprocal(o, a), [in_], [out])

    def rsqrt(self, out, in_):
        self.act(out, in_, AF.Sqrt)
        self.recip(out, out)

    def memset(self, out, val, eng="dve"):
        o = out.ap
        self.op(eng, lambda e: e.memset(o, val), [], [out])


class Arena:
    def __init__(self, nc, es, words, name="arena", rounded=False):
        self.rounded = rounded
        self.h = es.enter_context(nc.sbuf_tensor(name, [128, words], F32))
        self.words = words
        self.off = 0
        self.n = 0

    def alloc(self, w, name=None, parts=128):
        assert self.off + w <= self.words, f"arena overflow {self.off}+{w}>{self.words}"
        ap = self.h[0:parts, self.off:self.off + w]
        if self.rounded:
            ap = ap.bitcast(F32R)
        self.off += w
        self.n += 1
        return T(ap, name or f"buf{self.n}")

    def mark(self):
        return self.off

    def reset(self, to=0):
        self.off = to


class Ctx:
    pass


def layer_norm_tile(S_, x, g_bc, b_bc, st, tmp, width=D, eps=1e-5):
    S_.red(st[:, 0:1], x)
    S_.ts(st[:, 1:2], st[:, 0:1], -1.0 / width, ALU.mult)
    S_.act(x, x, AF.Identity, bias=st[:, 1:2])
    S_.tt(tmp, x, x, ALU.mult, eng="pool")
    S_.red(st[:, 2:3], tmp)
    S_.ts(st[:, 3:4], st[:, 2:3], 1.0 / width, ALU.mult, eps, ALU.add)
    S_.rsqrt(st[:, 3:4], st[:, 3:4])
    S_.stt(x, x, st[:, 3:4], g_bc, ALU.mult, ALU.mult)
    S_.tt(x, x, b_bc, ALU.add, eng="pool")


def sin_range(S_, A, out, ang, shift):
    n = ang.ap.shape[1]
    tmp = A.alloc(n)
    ki = A.alloc(n).cast(I32)
    kf = A.alloc(n)
    a2 = A.alloc(n)
    S_.ts(a2, ang, shift, ALU.add)
    S_.ts(tmp, a2, 1.0 / (2 * math.pi), ALU.mult)
    S_.copy(ki, tmp)
    S_.copy(kf, ki)
    S_.stt(tmp, kf, -2 * math.pi, a2, ALU.mult, ALU.add)
    S_.ts(kf, tmp, math.pi, ALU.is_ge)
    S_.stt(tmp, kf, -2 * math.pi, tmp, ALU.mult, ALU.add)
    S_.ts(tmp, tmp, math.pi, ALU.min, -math.pi, ALU.max)
    S_.act(out, tmp, AF.Sin)


def rope_apply(S_, x, a, r, cos, sin, tmps, H):
    x1 = x[:, :, a:a + r]
    x2 = x[:, :, a + r:a + 2 * r]
    c = cos.un(1).bc([128, H, r])
    s = sin.un(1).bc([128, H, r])
    t1, t2, t3, t4 = [t.re("p (h r) -> p h r", r=r) for t in tmps]
    S_.tt(t1, x1, c, ALU.mult)
    S_.tt(t2, x2, s, ALU.mult)
    S_.tt(t3, x2, c, ALU.mult, eng="pool")
    S_.tt(t4, x1, s, ALU.mult, eng="pool")
    S_.tt(x1, t1, t2, ALU.subtract)
    S_.tt(x2, t3, t4, ALU.add)


def rms_norm_tile(S_, x, g_bc, st, tmp, width, eps=1e-6):
    S_.tt(tmp, x, x, ALU.mult)
    S_.red(st[:, 0:1], tmp)
    S_.ts(st[:, 1:2], st[:, 0:1], 1.0 / width, ALU.mult, eps, ALU.add)
    S_.rsqrt(st[:, 1:2], st[:, 1:2])
    S_.stt(x, x, st[:, 1:2], g_bc, ALU.mult, ALU.mult)


def attention(C, l, qT_d, kT_d, v_d, o_d, dk, scale, biasT_d=None):
    S_, A, AR, ps, ident, triu = C.S_, C.A, C.AR, C.ps, C.ident, C.triu
    a0, r0 = A.mark(), AR.mark()
    qT = [AR.alloc(S, f"aqT{i}") for i in range(2)]
    kT = [AR.alloc(S, f"akT{i}") for i in range(2)]
    vv = [AR.alloc(NT * 66, f"av{i}").re("p (t e) -> p t e", e=66) for i in range(2)]
    pT = [AR.alloc(512, f"apT{i}") for i in range(3)]
    identr = AR.alloc(128, "identr")
    S_.dma(identr, C.ident_d.r)
    if biasT_d is not None:
        biasT = AR.alloc(S, "abiasT")
        S_.dma(biasT[0:64, :], biasT_d.r)
    ob = A.alloc(NT * 512, "aob").re("p (t c) -> p t c", c=512)
    rs = [A.alloc(1, f"ars{i}") for i in range(4)]
    pi = 0
    for h in range(8):
        b = h % 2
        S_.dma(qT[b][0:dk, :], qT_d[h].r)
        S_.dma(kT[b][0:dk, :], kT_d[h].r)
        S_.copy(vv[b][:, :, 64:66], C.ones[:, 0:2].un(1).bc([128, NT, 2]), eng="pool")
        S_.dma(vv[b][:, :, 0:64], T(v_d.ap[:, h, :].rearrange("(t p) e -> p t e", p=128), v_d.key).r)
        for G in range(4):
            nk = 4 * G + 4
            for kt in range(nk):
                sp = ps[kt % 2]
                S_.mm(sp, kT[b][0:dk, kt * 128:(kt + 1) * 128], qT[b][0:dk, G * 512:(G + 1) * 512],
                      start=True, stop=(biasT_d is None))
                if biasT_d is not None:
                    hn = h * 8 + kt // 2
                    S_.mm(sp, identr[0:64, hn:hn + 1].bc([64, 128]), biasT[0:64, G * 512:(G + 1) * 512],
                          start=False, stop=True)
                p = pT[pi % 3]
                pi += 1
                S_.act(p, sp, AF.Exp, scale=scale)
                for j in range(4):
                    qt = 4 * G + j
                    if qt < kt:
                        continue
                    if qt == kt:
                        S_.tt(p[:, j * 128:(j + 1) * 128], p[:, j * 128:(j + 1) * 128], triu, ALU.mult, eng="pool")
                    S_.mm(ps[2 + j][:, 0:66], p[:, j * 128:(j + 1) * 128], vv[b][:, kt, :], start=(kt == 0), stop=(kt == qt))
                    if kt == qt:
                        S_.recip(rs[j], ps[2 + j][:, 64:65])
                        S_.ts(ob[:, qt, h * 64:(h + 1) * 64], ps[2 + j][:, 0:64], rs[j], ALU.mult)
    for t in range(NT):
        S_.dma(o_d[t * 128:(t + 1) * 128, :], ob[:, t, :])
    S_.barrier()
    A.reset(a0)
    AR.reset(r0)
def mla_prep(C, l):
    S_, A, AR, ps, ident = C.S_, C.A, C.AR, C.ps, C.ident
    a0, r0 = A.mark(), AR.mark()
    wuq = AR.alloc(6 * 768, "wuq").re("p (c n) -> p c n", c=6)
    S_.dma(wuq, T(C.mla_w_uq.ap[l].rearrange("(c p) n -> p c n", p=128), "mla_w_uq").r)
    wukv = AR.alloc(2 * 1024, "wukv").re("p (c n) -> p c n", c=2)
    S_.dma(wukv, T(C.mla_w_ukv.ap[l].rearrange("(c p) n -> p c n", p=128), "mla_w_ukv").r)
    gq = A.alloc(768, "gq")
    S_.dma(gq, T(C.mla_q_norm.ap[l].partition_broadcast(128), "mla_q_norm"))
    gkv = A.alloc(256, "gkv")
    S_.dma(gkv, T(C.mla_kv_norm.ap[l].partition_broadcast(128), "mla_kv_norm"))
    cin = [A.alloc(1056, f"cin{i}") for i in range(2)]
    tmp = A.alloc(768, "mtmp")
    st = A.alloc(4, "mst")
    cT = AR.alloc(8 * 128, "cT").re("p (c t) -> p c t", c=8)
    qtm = A.alloc(768, "qtm")
    ktm = A.alloc(768, "ktm").re("p (h d) -> p h d", d=96)
    vtm = [A.alloc(512, f"vtm{i}").re("p (h e) -> p h e", e=64) for i in range(2)]
    rt = [A.alloc(8 * 16, f"rt{i}") for i in range(4)]
    qTs = [A.alloc(8 * 128, f"qTs{i}").re("p (h t) -> p h t", h=8) for i in range(2)]
    kTs = [A.alloc(8 * 128, f"kTs{i}").re("p (h t) -> p h t", h=8) for i in range(2)]
    for t in range(NT):
        b = t % 2
        ci = cin[b]
        S_.dma(ci, C.proj[t * 128:(t + 1) * 128, 0:1056].k((t, 0)))
        rms_norm_tile(S_, ci[:, 0:768], gq, st, tmp, 768)
        rms_norm_tile(S_, ci[:, 768:1024], gkv, st, tmp[:, 0:256], 256)
        for c in range(8):
            p = ps[6 + c % 2]
            S_.tr(p[:, 0:128], ci[:, c * 128:(c + 1) * 128], ident)
            S_.copy(cT[:, c, :], p[:, 0:128], eng=("dve" if c % 2 == 0 else "act"))
        for c in range(6):
            S_.mm(ps[0], cT[:, c, :], wuq[:, c, 0:512], start=(c == 0), stop=(c == 5))
        for c in range(6):
            S_.mm(ps[1][:, 0:256], cT[:, c, :], wuq[:, c, 512:768], start=(c == 0), stop=(c == 5))
        for c in range(2):
            S_.mm(ps[2], cT[:, 6 + c, :], wukv[:, c, 0:512], start=(c == 0), stop=(c == 1))
        for c in range(2):
            S_.mm(ps[3], cT[:, 6 + c, :], wukv[:, c, 512:1024], start=(c == 0), stop=(c == 1))
        S_.copy(qtm[:, 0:512], ps[0])
        S_.copy(qtm[:, 512:768], ps[1][:, 0:256], eng="act")
        q3 = qtm.re("p (h d) -> p h d", d=96)
        rope_apply(S_, q3, 64, 16, C.cosA[:, t, :], C.sinA[:, t, :], rt, 8)
        kr = ci[:, 1024:1056].re("p (h d) -> p h d", h=1)
        rope_apply(S_, kr, 0, 16, C.cosA[:, t, :], C.sinA[:, t, :], [x[:, 0:16] for x in rt], 1)
        for half in range(2):
            kv = ps[2 + half].re("p (h e) -> p h e", e=128)
            S_.copy(ktm[:, half * 4:(half + 1) * 4, 0:64], kv[:, :, 0:64])
            S_.copy(vtm[b][:, half * 4:(half + 1) * 4, :], kv[:, :, 64:128], eng="act")
        S_.copy(ktm[:, :, 64:96], kr.bc([128, 8, 32]), eng="pool")
        S_.dma(T(C.va.ap[t * 128:(t + 1) * 128], "va").k(t), vtm[b])
        for src, dst, dstd in ((q3, qTs[b], C.qTa), (ktm, kTs[b], C.kTa)):
            for h in range(8):
                p = ps[4 + h // 4]
                S_.tr(p[0:96, (h % 4) * 128:(h % 4 + 1) * 128], src[:, h, :], ident)
            S_.copy(dst[0:96, 0:4, :], ps[4][0:96, :].re("p (h t) -> p h t", h=4))
            S_.copy(dst[0:96, 4:8, :], ps[5][0:96, :].re("p (h t) -> p h t", h=4), eng="act")
            S_.dma(T(dstd.ap[:, :, t * 128:(t + 1) * 128].rearrange("h d t -> d h t"), dstd.key).k(t), dst[0:96, :, :])
    S_.barrier()
    A.reset(a0)
    AR.reset(r0)


def moba_prep(C, l):
    S_, A, AR, ps, ident = C.S_, C.A, C.AR, C.ps, C.ident
    a0, r0 = A.mark(), AR.mark()
    m = [A.alloc(1536, f"mb{i}") for i in range(2)]
    rt = [A.alloc(8 * 8, f"brt{i}") for i in range(4)]
    qTs = [A.alloc(8 * 128, f"bqTs{i}").re("p (h t) -> p h t", h=8) for i in range(2)]
    kTs = [A.alloc(8 * 128, f"bkTs{i}").re("p (h t) -> p h t", h=8) for i in range(2)]
    qTr = AR.alloc(8 * 128, "bqTr").re("p (h t) -> p h t", h=8)
    kmT = A.alloc(8 * 16, "kmT").re("p (h t) -> p h t", h=8)
    kmean = AR.alloc(64, "kmean").re("p (h n) -> p h n", h=8)
    S_.copy(kmean, C.zeros[:, 0:64].re("p (h n) -> p h n", h=8))
    gm = A.alloc(64, "gm").re("p (h n) -> p h n", h=8)
    top = A.alloc(64, "gtop").re("p (h n) -> p h n", h=8)
    bias = [A.alloc(64, f"gbias{i}").re("p (h n) -> p h n", h=8) for i in range(2)]
    bTs = [A.alloc(128, f"bTs{i}") for i in range(2)]
    for t in range(NT):
        b = t % 2
        own = t // 2
        mt = m[b]
        S_.dma(mt, C.proj[t * 128:(t + 1) * 128, O_MOBA:O_MOBA + 1536].k((t, 1)))
        q3 = mt[:, 0:512].re("p (h d) -> p h d", d=64)
        k3 = mt[:, 512:1024].re("p (h d) -> p h d", d=64)
        rope_apply(S_, q3, 0, 8, C.cosB[:, t, :], C.sinB[:, t, :], rt, 8)
        rope_apply(S_, k3, 0, 8, C.cosB[:, t, :], C.sinB[:, t, :], rt, 8)
        S_.dma(T(C.vb.ap[t * 128:(t + 1) * 128], "vb").k(t), mt[:, 1024:1536].re("p (h e) -> p h e", e=64))
        for src, dst, dstd in ((q3, qTs[b], C.qTb), (k3, kTs[b], C.kTb)):
            for h in range(8):
                p = ps[4 + h // 4]
                S_.tr(p[0:64, (h % 4) * 128:(h % 4 + 1) * 128], src[:, h, :], ident)
            S_.copy(dst[0:64, 0:4, :], ps[4][0:64, :].re("p (h t) -> p h t", h=4))
            S_.copy(dst[0:64, 4:8, :], ps[5][0:64, :].re("p (h t) -> p h t", h=4), eng="act")
            S_.dma(T(dstd.ap[:, :, t * 128:(t + 1) * 128].rearrange("h d t -> d h t"), dstd.key).k(t), dst[0:64, :, :])
        S_.red(kmT[0:64, :, t], kTs[b][0:64, :, :])
        bi = bias[b]
        if own >= 1:
            S_.copy(qTr[0:64], qTs[b][0:64], eng="pool")
            for h in range(8):
                S_.mm(ps[0][:, h * 8:(h + 1) * 8], qTr[0:64, h, :], kmean[0:64, h, :])
            S_.copy(gm, ps[0][:, 0:64].re("p (h n) -> p h n", h=8))
            S_.memset(gm[:, :, own:8], -1e30)
            for h in range(8):
                o_, i_ = top[:, h, :].ap, gm[:, h, :].ap
                S_.op("dve", (lambda o_, i_: (lambda e: e.max(out=o_, in_=i_)))(o_, i_), [gm], [top])
            S_.tt(bi, gm, top[:, :, 2:3].bc([128, 8, 8]), ALU.is_ge)
            S_.ts(bi, bi, -1.0, ALU.add, -NEG, ALU.mult)
        S_.memset(bi[:, :, own:own + 1], 0.0)
        if own < 7:
            S_.memset(bi[:, :, own + 1:8], NEG)
        S_.tr(ps[1][0:64, 0:128], bi.re("p h n -> p (h n)"), ident)
        S_.copy(bTs[b][0:64, :], ps[1][0:64, 0:128])
        S_.dma(C.biasTb[:, t * 128:(t + 1) * 128].k(t), bTs[b][0:64, :])
        if t % 2 == 1:
            n = t // 2
            S_.tt(kmean[0:64, :, n], kmT[0:64, :, t - 1], kmT[0:64, :, t], ALU.add)
            S_.ts(kmean[0:64, :, n], kmean[0:64, :, n], 1.0 / 256.0, ALU.mult)
    S_.barrier()
    A.reset(a0)
    AR.reset(r0)
def gdn_stage(C, l):
    S_, A, AR, ps, ident = C.S_, C.A, C.AR, C.ps, C.ident
    a0 = A.mark()
    P64 = 64
    i64 = ident[0:64, 0:64]

    def al(w, name):
        return A.alloc(w, name)[0:64, :]

    def hv(t):
        return t.re("p (h e) -> p h e", h=8)

    wconv_f = al(4 * 1536, "wconv")
    S_.dma(wconv_f, T(C.gdn_conv_w.ap[l].partition_broadcast(64), "gdn_conv_w"))
    wconv = wconv_f.re("p (j c) -> p j c", j=4)
    alog = al(8, "alog"); dtb = al(8, "dtb"); gno = al(64, "gno")
    S_.dma(alog, T(C.gdn_A_log.ap[l].partition_broadcast(64), "gdn_A_log"))
    S_.dma(dtb, T(C.gdn_dt_bias.ap[l].partition_broadcast(64), "gdn_dt_bias"))
    S_.dma(gno, T(C.gdn_o_norm.ap[l].partition_broadcast(64), "gdn_o_norm"))
    negA = al(8, "negA")
    S_.act(negA, alog, AF.Exp)
    S_.ts(negA, negA, -1.0, ALU.mult)
    ones = al(64, "ones64"); S_.memset(ones, 1.0)
    tri = al(64, "tri64"); S_.dma(tri, C.tri64_d)
    trilS = al(64, "trilS"); S_.dma(trilS, C.trilS_d)
    triuS = al(64, "triuS"); S_.dma(triuS, C.triuS_d)
    St = al(512, "gstate"); S_.memset(St, 0.0)
    xs2 = [al(1536, f"xs{j}") for j in range(2)]
    xs1 = [xs2[0], xs2[1], xs2[0], xs2[1]]
    xs = [xs1, xs1]
    y = al(1536, "gy"); ytmp = al(1536, "gytmp")
    zt1 = al(512, "gz")
    zt = [zt1, zt1]
    ab = [al(16, f"gab{b}") for b in range(2)]
    sm = al(64, "gsm")
    g_, beta, gcum, eg, egl, edec, rq = [sm[:, i * 8:(i + 1) * 8] for i in range(7)]
    ss = al(16, "gss")
    kb = al(512, "gkb"); vbt = al(512, "gvb"); kbg = al(512, "gkbg"); kdec = al(512, "gkdec")
    kT = al(512, "gkT"); qT = al(512, "gqT")
    dg = ytmp[:, 512:1024]; db = ytmp[:, 1024:1536]
    dd = al(512, "gdd"); dt_ = al(512, "gdt"); dtI = al(512, "gdtI")
    Pb = [al(512, f"gP{i}") for i in range(2)]
    Qb = [al(512, f"gQ{i}") for i in range(2)]
    R = al(512, "gR")
    u = al(512, "gu"); wT = al(512, "gwT"); qkT = al(512, "gqkT"); vnew = al(512, "gvnew")
    o = al(512, "go"); osq = ytmp[:, 0:512]; ot1 = al(512, "got"); ot = [ot1, ot1]
    p64 = [T(p.ap[0:64, :], p.key) for p in ps]

    def bc8(s):
        return s.un(2).bc([64, 8, 64])

    def bcm(mt):
        return mt.un(1).bc([64, 8, 64])

    for ci in range(min(S // 64, GDN_CHUNKS)):
        b = ci % 2
        c0 = ci * 64
        S_.dma(zt[b], C.proj[c0:c0 + 64, O_GZ:O_GZ + 512])
        S_.dma(ab[b], C.proj[c0:c0 + 64, O_GA:O_GA + 16])
        for j in range(4):
            sh = 3 - j
            xt = xs[b][j]
            if c0 - sh < 0:
                S_.memset(xt, 0.0)
                S_.dma(xt[sh:64, :], C.proj[0:64 - sh, O_GQKV:O_GQKV + 1536])
            else:
                S_.dma(xt, C.proj[c0 - sh:c0 - sh + 64, O_GQKV:O_GQKV + 1536])
            if j == 0:
                S_.tt(y, xt, wconv[:, 0, :], ALU.mult)
            else:
                S_.tt(ytmp, xt, wconv[:, j, :], ALU.mult, eng="pool")
                S_.tt(y, y, ytmp, ALU.add)
        S_.act(y, y, AF.Silu)
        if GDN_CUT == 1:
            continue
        S_.tt(ytmp[:, 0:1024], y[:, 0:1024], y[:, 0:1024], ALU.mult, eng="pool")
        S_.red(ss, ytmp[:, 0:1024].re("p (h e) -> p h e", e=64))
        S_.ts(ss, ss, 1e-6, ALU.add)
        S_.rsqrt(ss, ss)
        S_.ts(ss[:, 0:8], ss[:, 0:8], 0.125, ALU.mult)
        S_.tt(y[:, 0:1024].re("p (h e) -> p h e", e=64), y[:, 0:1024].re("p (h e) -> p h e", e=64),
              ss.un(2).bc([64, 16, 64]), ALU.mult)
        q3, k3, v3 = hv(y[:, 0:512]), hv(y[:, 512:1024]), hv(y[:, 1024:1536])
        S_.act(beta, ab[b][:, 8:16], AF.Sigmoid)
        S_.tt(g_, ab[b][:, 0:8], dtb, ALU.add)
        S_.act(g_, g_, AF.Exp)
        S_.act(g_, g_, AF.Ln, bias=1.0)
        S_.tt(g_, g_, negA, ALU.mult)
        S_.mm(p64[0][:, 0:8], tri, g_)
        S_.mm(p64[0][:, 8:16], ones, g_)
        S_.copy(gcum, p64[0][:, 0:8])
        S_.act(eg, p64[0][:, 0:8], AF.Exp)
        S_.act(egl, p64[0][:, 8:16], AF.Exp)
        S_.tt(edec, p64[0][:, 8:16], gcum, ALU.subtract)
        S_.act(edec, edec, AF.Exp)
        if GDN_CUT == 2:
            continue
        S_.tt(hv(kb), k3, bc8(beta), ALU.mult)
        S_.tt(hv(vbt), v3, bc8(beta), ALU.mult, eng="pool")
        S_.tt(hv(kbg), hv(kb), bc8(eg), ALU.mult)
        S_.tt(hv(kdec), k3, bc8(edec), ALU.mult, eng="pool")
        for h in range(8):
            S_.mm(p64[6][:, h * 64:(h + 1) * 64], k3[:, h, :], i64)
        for h in range(8):
            S_.mm(p64[7][:, h * 64:(h + 1) * 64], q3[:, h, :], i64)
        S_.copy(kT, p64[6])
        S_.copy(qT, p64[7], eng="act")
        if GDN_CUT == 3:
            continue
        S_.tt(hv(dg), bcm(i64), bc8(gcum), ALU.mult)
        S_.tt(hv(db), bcm(i64), bc8(beta), ALU.mult, eng="pool")
        S_.mm(p64[1], ones, dg)
        S_.mm(p64[2], ones, db)
        S_.tt(hv(dd), bc8(gcum), hv(p64[1]), ALU.subtract)
        S_.ts(dd, dd, 0.0, ALU.min)
        S_.act(dd, dd, AF.Exp)
        S_.tt(hv(dd), hv(dd), bcm(trilS), ALU.mult)
        S_.tt(hv(dt_), hv(p64[1]), bc8(gcum), ALU.subtract)
        S_.ts(dt_, dt_, 0.0, ALU.min)
        S_.act(dt_, dt_, AF.Exp)
        S_.tt(hv(dtI), hv(dt_), bcm(tri), ALU.mult, eng="pool")
        S_.tt(hv(dt_), hv(dt_), bcm(triuS), ALU.mult)
        if GDN_CUT == 4:
            continue
        for h in range(8):
            S_.mm(p64[3][:, h * 64:(h + 1) * 64], hv(kT)[:, h, :], hv(kT)[:, h, :])
        P, Q = Pb[0], Qb[0]
        S_.stt(P, p64[3], -1.0, dt_, ALU.mult, ALU.mult)
        S_.tt(P, P, p64[2], ALU.mult)
        S_.stt(Q, p64[3], -1.0, dd, ALU.mult, ALU.mult)
        S_.tt(hv(Q), hv(Q), bc8(beta), ALU.mult)
        S_.tt(hv(R), hv(P), bcm(i64), ALU.add)
        if GDN_CUT == 5:
            continue
        for lv in range(1, 6):
            Pn, Qn = Pb[lv % 2], Qb[lv % 2]
            if lv < 5:
                for h in range(8):
                    S_.mm(p64[4][:, h * 64:(h + 1) * 64], hv(Q)[:, h, :], hv(P)[:, h, :])
            for h in range(8):
                S_.mm(p64[5][:, h * 64:(h + 1) * 64], hv(P)[:, h, :], hv(Q)[:, h, :])
            if lv < 5:
                S_.copy(Pn, p64[4])
            S_.copy(Qn, p64[5], eng="act")
            for h in range(8):
                S_.mm(p64[0][:, h * 64:(h + 1) * 64], hv(Qn)[:, h, :], hv(R)[:, h, :])
            S_.tt(R, R, p64[0], ALU.add)
            P, Q = Pn, Qn
        if GDN_CUT == 6:
            continue
        for h in range(8):
            S_.mm(p64[1][:, h * 64:(h + 1) * 64], hv(R)[:, h, :], hv(vbt)[:, h, :])
        for h in range(8):
            S_.mm(p64[2][:, h * 64:(h + 1) * 64], hv(kbg)[:, h, :], hv(R)[:, h, :])
        for h in range(8):
            S_.mm(p64[3][:, h * 64:(h + 1) * 64], hv(kT)[:, h, :], hv(qT)[:, h, :])
        S_.copy(u, p64[1])
        S_.copy(wT, p64[2], eng="act")
        S_.tt(qkT, p64[3], dtI, ALU.mult)
        if GDN_CUT == 7:
            continue
        for h in range(8):
            S_.mm(p64[4][:, h * 64:(h + 1) * 64], hv(wT)[:, h, :], hv(St)[:, h, :])
        for h in range(8):
            S_.mm(p64[5][:, h * 64:(h + 1) * 64], hv(qT)[:, h, :], hv(St)[:, h, :])
        S_.tt(vnew, u, p64[4], ALU.subtract)
        for h in range(8):
            S_.mm(p64[6][:, h * 64:(h + 1) * 64], hv(qkT)[:, h, :], hv(vnew)[:, h, :])
        for h in range(8):
            S_.mm(p64[7][:, h * 64:(h + 1) * 64], hv(kdec)[:, h, :], hv(vnew)[:, h, :])
        S_.tt(hv(o), hv(p64[5]), bc8(eg), ALU.mult)
        S_.tt(o, o, p64[6], ALU.add)
        S_.tt(hv(St), hv(St), bc8(egl), ALU.mult)
        S_.tt(St, St, p64[7], ALU.add)
        if GDN_CUT == 8:
            continue
        S_.tt(osq, o, o, ALU.mult, eng="pool")
        S_.red(rq, hv(osq))
        S_.ts(rq, rq, 1.0 / 64, ALU.mult, 1e-6, ALU.add)
        S_.rsqrt(rq, rq)
        S_.tt(hv(o), hv(o), bc8(rq), ALU.mult)
        S_.tt(hv(o), hv(o), gno.un(1).bc([64, 8, 64]), ALU.mult)
        S_.act(zt[b], zt[b], AF.Silu)
        S_.tt(ot[b], o, zt[b], ALU.mult)
        S_.dma(C.oc[c0:c0 + 64, :].k(ci), ot[b])
    S_.barrier()
    A.reset(a0)
def merge_stage(C, l):
    S_, A, AR, ps, ident = C.S_, C.A, C.AR, C.ps, C.ident
    a0, r0 = A.mark(), AR.mark()
    wbr = AR.alloc(12 * 1024, "wbr").re("p (n c d) -> p n c d", n=3, c=4)
    for n in range(3):
        S_.dma(wbr[:, n], T(C.w_branch.ap[l, n].rearrange("(c p) d -> p c d", p=128), "w_branch").r)
    wout = AR.alloc(8 * 1024, "wout").re("p (c d) -> p c d", c=8)
    S_.dma(wout, T(C.w_out.ap[l].rearrange("(c p) d -> p c d", p=128), "w_out").r)
    oT = AR.alloc(12 * 128, "oT").re("p (c t) -> p c t", c=12)
    mT = AR.alloc(8 * 128, "mT").re("p (c t) -> p c t", c=8)
    gb = A.alloc(3072, "gb"); S_.dma(gb, T(C.gate_bias.ap[l].partition_broadcast(128), "gate_bias"))
    g1 = A.alloc(D, "g1"); S_.dma(g1, T(C.ln1_g.ap[l].partition_broadcast(128), "ln1_g"))
    b1 = A.alloc(D, "b1"); S_.dma(b1, T(C.ln1_b.ap[l].partition_broadcast(128), "ln1_b"))
    o3 = A.alloc(1536, "o3"); gl = A.alloc(3072, "gl"); mg = A.alloc(D, "mg"); tmp = A.alloc(D, "mtmp2")
    hh = A.alloc(D, "hh"); st = A.alloc(4, "mst2")
    for t in range(min(NT, NT_LIMIT)):
        r0_, r1_ = t * 128, (t + 1) * 128
        S_.dma(o3[:, 0:512], C.oa[r0_:r1_, :])
        S_.dma(o3[:, 512:1024], C.ob[r0_:r1_, :])
        S_.dma(o3[:, 1024:1536], C.oc[r0_:r1_, :])
        S_.dma(gl, C.proj[r0_:r1_, O_GATE:O_GATE + 3072])
        S_.dma(hh, C.hbuf[r0_:r1_, :].k(t))
        for c in range(12):
            p = ps[6 + c % 2]
            S_.tr(p[:, 0:128], o3[:, c * 128:(c + 1) * 128], ident)
            S_.copy(oT[:, c, :], p[:, 0:128], eng=("dve" if c % 2 == 0 else "act"))
        S_.tt(gl, gl, gb, ALU.add, eng="pool")
        S_.act(gl, gl, AF.Sigmoid)
        for n in range(3):
            for half in range(2):
                for c in range(4):
                    S_.mm(ps[n * 2 + half], oT[:, n * 4 + c, :], wbr[:, n, c, half * 512:(half + 1) * 512],
                          start=(c == 0), stop=(c == 3))
        for half in range(2):
            hs = slice(half * 512, (half + 1) * 512)
            S_.tt(mg[:, hs], gl[:, half * 512:(half + 1) * 512], ps[half], ALU.mult)
            for n in (1, 2):
                S_.tt(tmp[:, hs], gl[:, n * 1024 + half * 512:n * 1024 + (half + 1) * 512], ps[n * 2 + half], ALU.mult)
                S_.tt(mg[:, hs], mg[:, hs], tmp[:, hs], ALU.add, eng="pool")
        for c in range(8):
            p = ps[6 + c % 2]
            S_.tr(p[:, 0:128], mg[:, c * 128:(c + 1) * 128], ident)
            S_.copy(mT[:, c, :], p[:, 0:128], eng=("dve" if c % 2 == 0 else "act"))
        for half in range(2):
            for c in range(8):
                S_.mm(ps[half], mT[:, c, :], wout[:, c, half * 512:(half + 1) * 512], start=(c == 0), stop=(c == 7))
        for half in range(2):
            hs = slice(half * 512, (half + 1) * 512)
            S_.stt(hh[:, hs], hh[:, hs], ALPHA, ps[half], ALU.mult, ALU.add)
        layer_norm_tile(S_, hh, g1, b1, st, tmp)
        S_.dma(C.hbuf[r0_:r1_, :].k(t), hh)
    S_.barrier()
    A.reset(a0)
    AR.reset(r0)


def peer_stage(C, l):
    S_, A, AR, ps, ident = C.S_, C.A, C.AR, C.ps, C.ident
    a0, r0 = A.mark(), AR.mark()
    NTL = min(NT, NT_LIMIT)
    hT = AR.alloc(8 * S, "phT").re("p (c t) -> p c t", c=8)
    ht = [A.alloc(D, f"pht{i}") for i in range(2)]
    for t in range(NTL):
        b = t % 2
        S_.dma(ht[b], C.hbuf[t * 128:(t + 1) * 128, :].k(t))
        for c in range(8):
            p = ps[6 + c % 2]
            S_.tr(p[:, 0:128], ht[b][:, c * 128:(c + 1) * 128], ident)
            S_.copy(hT[:, c, t * 128:(t + 1) * 128], p[:, 0:128], eng=("dve" if c % 2 == 0 else "act"))
    wq = [AR.alloc(8 * 128, f"pwq{i}").re("p (c n) -> p c n", c=8) for i in range(2)]
    sk = [AR.alloc(128, f"psk{i}") for i in range(2)]
    qT = [AR.alloc(S, f"pqT{i}") for i in range(2)]
    sct = [A.alloc(512, f"psct{i}") for i in range(2)]
    wq_l = T(C.peer_w_q.ap[l].rearrange("(c p) n -> p c n", p=128), "peer_w_q")
    NG = (NTL + 3) // 4
    for hp in range(16):
        b = hp % 2
        S_.dma(wq[b], wq_l[:, :, hp * 128:(hp + 1) * 128].r)
        S_.dma(sk[b], T(C.peer_skT.ap[l, hp], "peer_skT").r)
        for g in range(NG):
            p = ps[g % 2]
            for c in range(8):
                S_.mm(p, wq[b][:, c, :], hT[:, c, g * 512:(g + 1) * 512], start=(c == 0), stop=(c == 7))
            S_.copy(qT[b][:, g * 512:(g + 1) * 512], p, eng=("dve" if g % 2 == 0 else "act"))
        for g in range(NG):
            p = ps[2 + g % 2]
            for j in range(4):
                tt_ = g * 4 + j
                S_.mm(p[:, j * 128:(j + 1) * 128], qT[b][:, tt_ * 128:(tt_ + 1) * 128], sk[b])
            S_.copy(sct[g % 2], p, eng=("dve" if g % 2 == 0 else "act"))
            S_.dma(T(C.sc.ap[g * 512:(g + 1) * 512, hp, :].rearrange("(j p) n -> p j n", p=128), "sc").k((g, hp)),
                   sct[g % 2].re("p (j n) -> p j n", j=4))
    S_.barrier()
    A.reset(a0)
    AR.reset(r0)
    g2 = A.alloc(D, "g2"); S_.dma(g2, T(C.ln2_g.ap[l].partition_broadcast(128), "ln2_g"))
    b2 = A.alloc(D, "b2"); S_.dma(b2, T(C.ln2_b.ap[l].partition_broadcast(128), "ln2_b"))
    sc3 = A.alloc(2048, "sc3")
    wk = A.alloc(128, "pwk")
    stop_ = A.alloc(256, "pstop").re("p (h k) -> p h k", k=16)
    itop = A.alloc(256, "pitop").cast(U32).re("p (h k) -> p h k", k=16)
    itopf = A.alloc(256, "pitopf")
    cand = A.alloc(256, "pcand"); cwk = A.alloc(256, "pcwk")
    cs = A.alloc(128, "pcs").re("p (h k) -> p h k", k=16)
    cpos = A.alloc(128, "pcpos").cast(U32).re("p (h k) -> p h k", k=16)
    ai = A.alloc(128, "pai").cast(I32); bi = A.alloc(128, "pbi").cast(I32)
    af = A.alloc(128, "paf"); bf = A.alloc(128, "pbf")
    eq = A.alloc(2048, "peq")
    i1s = A.alloc(128, "pi1s"); i2s = A.alloc(128, "pi2s"); idf = A.alloc(128, "pidf")
    idx2 = [A.alloc(128, f"pidx{i}").cast(I32) for i in range(2)]
    ee = A.alloc(128, "pee"); se = A.alloc(8, "pse")
    gate2 = [A.alloc(128, f"pgate{i}") for i in range(2)]
    actv = A.alloc(128, "pactv"); coef = A.alloc(128, "pcoef")
    xt2 = [A.alloc(D, f"pxt{i}") for i in range(2)]
    acc = A.alloc(D, "pacc"); accb = A.alloc(D, "paccb"); tmp = A.alloc(D, "ptmp"); st = A.alloc(4, "pst")
    NB = 8
    gbuf = [A.alloc(D, f"pg{i}") for i in range(NB)]
    junk = [A.alloc(D, f"pj{i}") for i in range(2)]
    nrows = DEPTH * 16384

    def vmax(o, i):
        oa, ia = o.ap, i.ap
        S_.op("dve", lambda e: e.max(out=oa, in_=ia), [i], [o])

    def vmaxidx(o, m, i):
        oa, ma, ia = o.ap, m.ap, i.ap
        S_.op("dve", lambda e: e.max_index(oa, ma, ia), [m, i], [o])

    def vrepl(o, m, i):
        oa, ma, ia = o.ap, m.ap, i.ap
        S_.op("dve", lambda e: e.match_replace(out=oa, in_to_replace=ma, in_values=ia, imm_value=-1e30), [m, i], [o])

    def gather(dst, table, s, idx):
        o_, t_, i_ = dst.ap, table.ap, idx[:, s:s + 1].ap
        S_.dma(dst, table, eng="pool", extra_reads=[idx],
               fn=lambda e: e.indirect_dma_start(out=o_, out_offset=None, in_=t_,
                                                 in_offset=bass.IndirectOffsetOnAxis(ap=i_, axis=0),
                                                 bounds_check=S_.bnd_reg, oob_is_err=False))

    def select(t):
        r0_, r1_ = t * 128, (t + 1) * 128
        idx, gate, xt = idx2[t % 2], gate2[t % 2], xt2[t % 2]
        S_.dma(sc3, T(C.sc.ap[r0_:r1_].rearrange("p h n -> p (h n)"), "sc"))
        S_.dma(xt, C.hbuf[r0_:r1_, :].k(t))
        for hp in range(16):
            v = sc3[:, hp * 128:(hp + 1) * 128]
            vmax(stop_[:, hp, 0:8], v)
            vmaxidx(itop[:, hp, 0:8], stop_[:, hp, 0:8], v)
            vrepl(wk, stop_[:, hp, 0:8], v)
            vmax(stop_[:, hp, 8:16], wk)
            vmaxidx(itop[:, hp, 8:16], stop_[:, hp, 8:16], wk)
        S_.copy(itopf, itop.re("p h k -> p (h k)"))
        s4 = stop_.re("p (h q) k -> p h q k", q=2)
        c3 = cand.re("p (a b) -> p a b", b=16)
        for h in range(8):
            S_.tt(c3, s4[:, h, 0, :].un(2).bc([128, 16, 16]), s4[:, h, 1, :].un(1).bc([128, 16, 16]), ALU.add)
            vmax(cs[:, h, 0:8], cand)
            vmaxidx(cpos[:, h, 0:8], cs[:, h, 0:8], cand)
            vrepl(cwk, cs[:, h, 0:8], cand)
            vmax(cs[:, h, 8:16], cwk)
            vmaxidx(cpos[:, h, 8:16], cs[:, h, 8:16], cwk)
        cpf = cpos.re("p h k -> p (h k)").cast(I32)
        ca, cpa = ai.ap, cpf.ap
        S_.op("dve", lambda e: e.tensor_single_scalar(ca, cpa, 4, ALU.logical_shift_right), [cpos], [ai])
        cb = bi.ap
        S_.op("dve", lambda e: e.tensor_single_scalar(cb, cpa, 15, ALU.bitwise_and), [cpos], [bi])
        S_.copy(af, ai)
        S_.copy(bf, bi)
        i4 = itopf.re("p (h q k) -> p h q k", q=2, k=16)
        eq4 = eq.re("p (h k a) -> p h k a", h=8, k=16)
        iota_bc = C.iota16.un(1).un(1).bc([128, 8, 16, 16])
        for src, q, dst in ((af, 0, i1s), (bf, 1, i2s)):
            S_.tt(eq4, iota_bc, src.re("p (h k) -> p h k", k=16).un(3).bc([128, 8, 16, 16]), ALU.is_equal)
            S_.tt(eq4, eq4, i4[:, :, q, :].un(2).bc([128, 8, 16, 16]), ALU.mult)
            S_.red(dst, eq.re("p (s a) -> p s a", a=16))
        S_.stt(idf, i1s, 128.0, i2s, ALU.mult, ALU.add)
        if l > 0:
            S_.ts(idf, idf, float(l * 16384), ALU.add)
        S_.copy(idx, idf)
        cs2 = cs
        S_.tt(ee.re("p (h k) -> p h k", k=16), cs2, cs2[:, :, 0:1].bc([128, 8, 16]), ALU.subtract)
        S_.act(ee, ee, AF.Exp)
        S_.red(se, ee.re("p (h k) -> p h k", k=16))
        S_.recip(se, se)
        S_.tt(gate.re("p (h k) -> p h k", k=16), ee.re("p (h k) -> p h k", k=16), se.un(2).bc([128, 8, 16]), ALU.mult)
    def experts(t):
        r0_, r1_ = t * 128, (t + 1) * 128
        idx, gate, xt = idx2[t % 2], gate2[t % 2], xt2[t % 2]
        for s in range(128 + NB - 1):
            if s < 128:
                gather(gbuf[s % NB], C.peer_u, s, idx)
            s2 = s - (NB - 1)
            if s2 >= 0:
                j_, x_, u_, a_ = junk[s2 % 2].ap, xt.ap, gbuf[s2 % NB].ap, actv[:, s2:s2 + 1].ap
                S_.op("dve", (lambda j_, x_, u_, a_: (lambda e: e.scalar_tensor_tensor(j_, x_, 1.0, u_, ALU.mult, ALU.mult, accum_out=a_)))(j_, x_, u_, a_),
                      [xt, gbuf[s2 % NB]], [junk[s2 % 2], actv.k(s2)])
        co_, av_ = coef.ap, actv.ap
        S_.op("act", (lambda co_, av_: (lambda e: e.activation(co_, av_, AF.Gelu)))(co_, av_),
              [actv.k(s_) for s_ in range(128)], [coef])
        S_.tt(coef, coef, gate, ALU.mult)
        S_.memset(acc, 0.0)
        S_.memset(accb, 0.0, eng="pool")
        for s in range(128 + NB - 1):
            if s < 128:
                gather(gbuf[s % NB], C.peer_v, s, idx)
            s2 = s - (NB - 1)
            if s2 >= 0:
                ac_ = acc if s2 % 2 == 0 else accb
                S_.stt(ac_, gbuf[s2 % NB], coef[:, s2:s2 + 1], ac_, ALU.mult, ALU.add)
        S_.tt(acc, acc, accb, ALU.add)
        S_.stt(xt, xt, ALPHA, acc, ALU.mult, ALU.add)
        layer_norm_tile(S_, xt, g2, b2, st, tmp)
        S_.dma(C.hbuf[r0_:r1_, :].k(t), xt)
    select(0)
    for t in range(NTL):
        if t + 1 < NTL:
            select(t + 1)
        experts(t)
    S_.barrier()
    A.reset(a0)
    AR.reset(r0)


def build(debug_outs=(), stages=None, depth=DEPTH):
    nc = bass.Bass("TRN2", target_bir_lowering=False)
    nc.dge_precook = False
    es = ExitStack()
    C = Ctx()
    C.nc = nc

    C.in_names = []

    def din(name, shape, dt=F32):
        C.in_names.append(name)
        return T(nc.dram_tensor(name, list(shape), dt, kind="ExternalInput").ap(), name)

    def dscr(name, shape, dt=F32):
        kind = "ExternalOutput" if name in debug_outs else "Internal"
        return T(nc.dram_tensor(name, list(shape), dt, kind=kind).ap(), name)

    x = din("x", [S, D])
    positions = din("positions", [S], I32)
    ln_in_g = din("ln_in_g", [D]); ln_in_b = din("ln_in_b", [D])
    w_in = din("w_in", [DEPTH, D, IN_COLS])
    C.mla_q_norm = din("mla_q_norm", [DEPTH, 768]); C.mla_kv_norm = din("mla_kv_norm", [DEPTH, 256])
    C.mla_w_uq = din("mla_w_uq", [DEPTH, 768, 768]); C.mla_w_ukv = din("mla_w_ukv", [DEPTH, 256, 1024])
    C.gdn_conv_w = din("gdn_conv_w", [DEPTH, 4 * 1536]); C.gdn_A_log = din("gdn_A_log", [DEPTH, 8])
    C.gdn_dt_bias = din("gdn_dt_bias", [DEPTH, 8]); C.gdn_o_norm = din("gdn_o_norm", [DEPTH, 64])
    C.gate_bias = din("gate_bias", [DEPTH, 3 * D]); C.w_branch = din("w_branch", [DEPTH, 3, 512, D])
    C.w_out = din("w_out", [DEPTH, D, D])
    C.ln1_g = din("ln1_g", [DEPTH, D]); C.ln1_b = din("ln1_b", [DEPTH, D])
    C.peer_w_q = din("peer_w_q", [DEPTH, D, 2048]); C.peer_skT = din("peer_skT", [DEPTH, 16, 128, 128])
    if stages is None or any(s_.startswith("peer") for s_ in stages):
        C.peer_u = din("peer_u", [DEPTH * 16384, D]); C.peer_v = din("peer_v", [DEPTH * 16384, D])
    C.ln2_g = din("ln2_g", [DEPTH, D]); C.ln2_b = din("ln2_b", [DEPTH, D])
    C.ident_d = din("ident", [128, 128])
    triu_d = din("triu", [128, 128])
    invf_d = din("invf", [128, 24])
    C.tri64_d = din("tri64", [64, 64]); C.trilS_d = din("trilS", [64, 64]); C.triuS_d = din("triuS", [64, 64])
    iota16_d = din("iota16", [128, 16])
    out = T(nc.dram_tensor("out", [S, D], F32, kind="ExternalOutput").ap(), "out")
    hbuf = dscr("hbuf", [S, D]); C.hbuf = hbuf
    C.proj = dscr("proj", [S, IN_COLS])
    C.qTa = dscr("qTa", [8, 96, S]); C.kTa = dscr("kTa", [8, 96, S]); C.va = dscr("va", [S, 8, 64]); C.oa = dscr("oa", [S, 512])
    C.qTb = dscr("qTb", [8, 64, S]); C.kTb = dscr("kTb", [8, 64, S]); C.vb = dscr("vb", [S, 8, 64]); C.ob = dscr("ob", [S, 512])
    C.biasTb = dscr("biasTb", [64, S])
    C.oc = dscr("oc", [S, 512])
    C.sc = dscr("sc", [S, 16, 128])

    S_ = Sched(nc, es); C.S_ = S_
    A = Arena(nc, es, 28000); C.A = A
    AR = Arena(nc, es, 25000, name="arena_r", rounded=True); C.AR = AR
    ps = [T(es.enter_context(nc.psum_tensor(f"ps{i}", [128, 512], F32))[:, :], f"ps{i}") for i in range(8)]
    C.ps = ps

    def want(name):
        return stages is None or name in stages

    ident = A.alloc(128, "ident"); S_.dma(ident, C.ident_d); C.ident = ident
    triu = A.alloc(128, "triu"); S_.dma(triu, triu_d); C.triu = triu
    C.iota16 = A.alloc(16, "iota16"); S_.dma(C.iota16, iota16_d)
    C.ones = A.alloc(64, "ones"); S_.memset(C.ones, 1.0)
    C.zeros = A.alloc(64, "zeros"); S_.memset(C.zeros, 0.0)
    rope = A.alloc(2 * NT * 24, "rope").re("p (k t j) -> p k t j", k=2, t=NT)
    sinT, cosT = rope[:, 0], rope[:, 1]
    C.cosA, C.sinA = cosT[:, :, 0:16], sinT[:, :, 0:16]
    C.cosB, C.sinB = cosT[:, :, 16:24], sinT[:, :, 16:24]
    base = A.mark()
    if want("init"):
        pos_i = A.alloc(NT, "pos_i").cast(I32)
        for t in range(NT):
            S_.dma(pos_i[:, t:t + 1], T(positions.ap[t * 128:(t + 1) * 128].rearrange("(p o) -> p o", o=1), "positions"))
        posf = A.alloc(NT, "posf"); invf = A.alloc(24, "invf")
        S_.dma(invf, invf_d)
        S_.copy(posf, pos_i)
        ang = A.alloc(NT * 24, "ang")
        S_.tt(ang.re("p (t j) -> p t j", j=24), posf.un(2).bc([128, NT, 24]), invf.un(1).bc([128, NT, 24]), ALU.mult)
        sin_range(S_, A, sinT.re("p t j -> p (t j)"), ang, 0.0)
        sin_range(S_, A, cosT.re("p t j -> p (t j)"), ang, math.pi / 2)
        S_.barrier()
        A.reset(base)

    if want("ln_in"):
        g_bc = A.alloc(D, "g_bc"); b_bc = A.alloc(D, "b_bc")
        S_.dma(g_bc, T(ln_in_g.ap.partition_broadcast(128), "ln_in_g"))
        S_.dma(b_bc, T(ln_in_b.ap.partition_broadcast(128), "ln_in_b"))
        xt = [A.alloc(D, f"xt{i}") for i in range(2)]
        tmp = A.alloc(D, "lntmp")
        st = [A.alloc(4, f"st{i}") for i in range(2)]
        for t in range(NT):
            b = t % 2
            S_.dma(xt[b], x[t * 128:(t + 1) * 128, :])
            layer_norm_tile(S_, xt[b], g_bc, b_bc, st[b], tmp)
            S_.dma(hbuf[t * 128:(t + 1) * 128, :].k(t), xt[b])
        S_.barrier()
        A.reset(base)

    for l in range(depth):
        if want(f"proj{l}"):
            hT = AR.alloc(8 * S, "hT").re("p (c t) -> p c t", c=8)
            ht = [A.alloc(D, f"ht{i}") for i in range(2)]
            for t in range(NT):
                b = t % 2
                S_.dma(ht[b], hbuf[t * 128:(t + 1) * 128, :].k(t))
                for c in range(8):
                    p = ps[c % 2]
                    S_.tr(p[:, 0:128], ht[b][:, c * 128:(c + 1) * 128], ident)
                    S_.copy(hT[:, c, t * 128:(t + 1) * 128].k((c, t)), p[:, 0:128], eng=("dve" if c % 2 == 0 else "act"))
            wt = [AR.alloc(8 * 512, f"wt{i}").re("p (c n) -> p c n", c=8) for i in range(2)]
            ot = [A.alloc(512, f"ot{i}") for i in range(4)]
            w_l = T(w_in.ap[l].rearrange("(c p) n -> p c n", p=128), "w_in")
            ncg = (IN_COLS + 511) // 512
            oi = 0
            for cg in range(ncg):
                c0 = cg * 512
                cw = min(512, IN_COLS - c0)
                wb = wt[cg % 2]
                S_.dma(wb[:, :, 0:cw], w_l[:, :, c0:c0 + cw].r)
                for t in range(NT):
                    p = ps[2 + (t % 4)]
                    for c in range(8):
                        S_.mm(p[:, 0:cw], hT[:, c, t * 128:(t + 1) * 128].k((c, t)), wb[:, c, 0:cw],
                              start=(c == 0), stop=(c == 7))
                    o = ot[oi % 4]
                    oi += 1
                    S_.copy(o[:, 0:cw], p[:, 0:cw], eng=("dve" if t % 2 == 0 else "act"))
                    S_.dma(C.proj[t * 128:(t + 1) * 128, c0:c0 + cw].k((t, cg)), o[:, 0:cw])
            S_.barrier()
            A.reset(base)
            AR.reset(0)
        if want(f"mla{l}"):
            mla_prep(C, l)
            attention(C, l, C.qTa, C.kTa, C.va, C.oa, 96, 96 ** -0.5)
        if want(f"moba{l}"):
            moba_prep(C, l)
            attention(C, l, C.qTb, C.kTb, C.vb, C.ob, 64, 0.125, biasT_d=C.biasTb)
        if want(f"gdn{l}"):
            gdn_stage(C, l)
        if want(f"merge{l}"):
            merge_stage(C, l)
        if want(f"peer{l}"):
            peer_stage(C, l)

    if want("final"):
        ft = [A.alloc(D, f"ft{i}") for i in range(2)]
        for t in range(NT):
            S_.dma(ft[t % 2], hbuf[t * 128:(t + 1) * 128, :].k(t))
            S_.dma(out[t * 128:(t + 1) * 128, :], ft[t % 2])
        S_.barrier()

    S_.emit()
    nc._keep = (es, S_, A, AR)
    nc._in_names = list(C.in_names)
    return nc


def make_in_maps(inputs):
    n = 8
    f32 = np.float32
    i128 = np.arange(128)
    i64 = np.arange(64)
    invfA = 500000.0 ** (-np.arange(0, 32, 2, dtype=f32) / f32(32))
    invfB = 500000.0 ** (-np.arange(0, 16, 2, dtype=f32) / f32(16))
    consts = {
        "ident": np.eye(128, dtype=f32),
        "triu": (i128[None, :] >= i128[:, None]).astype(f32),
        "invf": np.tile(np.concatenate([invfA, invfB]).astype(f32)[None, :], (128, 1)),
        "tri64": (i64[:, None] <= i64[None, :]).astype(f32),
        "trilS": (i64[None, :] < i64[:, None]).astype(f32),
        "triuS": (i64[None, :] > i64[:, None]).astype(f32),
        "iota16": np.tile(np.arange(16, dtype=f32)[None, :], (128, 1)),
    }
    shared = {}
    for k in ("ln_in_g", "ln_in_b", "w_in", "mla_q_norm", "mla_kv_norm", "mla_w_uq", "mla_w_ukv", "gdn_A_log",
              "gdn_dt_bias", "gdn_o_norm", "w_branch", "w_out", "ln1_g", "ln1_b", "peer_w_q", "ln2_g", "ln2_b"):
        shared[k] = np.ascontiguousarray(inputs[k], dtype=f32)
    shared["gdn_conv_w"] = np.ascontiguousarray(inputs["gdn_conv_w"], dtype=f32).reshape(DEPTH, 4 * 1536)
    shared["gate_bias"] = np.ascontiguousarray(inputs["gate_bias"], dtype=f32).reshape(DEPTH, 3 * D)
    sk = np.asarray(inputs["peer_sub_keys"], dtype=f32)
    shared["peer_skT"] = np.ascontiguousarray(sk.transpose(0, 1, 2, 4, 3).reshape(DEPTH, 16, 128, 128))
    shared["peer_u"] = np.ascontiguousarray(inputs["peer_u"], dtype=f32).reshape(DEPTH * 16384, D)
    shared["peer_v"] = np.ascontiguousarray(inputs["peer_v"], dtype=f32).reshape(DEPTH * 16384, D)
    maps = []
    for b in range(n):
        m = dict(shared)
        m.update(consts)
        m["x"] = np.ascontiguousarray(inputs["x"][b], dtype=f32)
        m["positions"] = np.ascontiguousarray(inputs["positions"][b], dtype=np.int32)
        maps.append(m)
    return maps


def kernel(**inputs):
    inputs = {k: np.asarray(v) for k, v in inputs.items()}
    nc = build()
    res = run_bass_kernel_spmd(nc, make_in_maps(inputs), core_ids=list(range(8)))
    return np.stack([np.asarray(r["out"], dtype=np.float32) for r in res.results], axis=0)
```

```python
import math
import threading
from contextlib import ExitStack

import numpy as np
import concourse.bass as bass
import concourse.mybir as mybir
from concourse.bass_utils import run_bass_kernel_spmd

F32 = mybir.dt.float32
F32R = mybir.dt.float32r
I32 = mybir.dt.int32
U32 = mybir.dt.uint32
ALU = mybir.AluOpType
AF = mybir.ActivationFunctionType
AX = mybir.AxisListType

D = 1024
S = 2048
NT = S // 128
DEPTH = 2
IN_SPLITS = (768, 256, 32, 1536, 1536, 512, 8, 8, 3072)
IN_COLS = sum(IN_SPLITS)
OFF = np.cumsum((0,) + IN_SPLITS).tolist()
O_CQ, O_CKV, O_KR, O_MOBA, O_GQKV, O_GZ, O_GA, O_GB, O_GATE = OFF[:9]
ALPHA = (2 * DEPTH) ** 0.25
NEG = -30000.0
ENGS = ("pe", "dve", "act", "pool", "sp")
GDN_CHUNKS = 32
NT_LIMIT = 16
GDN_CUT = 0


class T:
    def __init__(self, ap, key):
        self.ap, self.key = ap, key

    def __getitem__(self, idx):
        return T(self.ap[idx], self.key)

    def k(self, sub):
        base = self.key[0] if isinstance(self.key, tuple) else self.key
        return T(self.ap, (base, sub))

    def re(self, pat, **kw):
        return T(self.ap.rearrange(pat, **kw), self.key)

    def bc(self, shape):
        return T(self.ap.broadcast_to(list(shape)), self.key)

    def un(self, axis):
        return T(self.ap.unsqueeze(axis), self.key)

    def cast(self, dt):
        return T(self.ap.bitcast(dt), self.key)

    @property
    def r(self):
        return T(self.ap.bitcast(F32R), self.key)


def _keys(ts):
    out = []
    for t in ts:
        if isinstance(t, T):
            out.append(t.key)
    return out


class Sched:
    def __init__(self, nc, es, n_dsem=44):
        self.nc = nc
        self.sem = {e: es.enter_context(nc.semaphore("sem_" + e)) for e in ENGS}
        self.dsem = [es.enter_context(nc.semaphore(f"dsem{i}")) for i in range(n_dsem)]
        self.dtgt = [0] * n_dsem
        self.dnext = 0
        self.dnext_sw = 0
        self.n_hw = n_dsem - 12
        self.ins = {e: [] for e in ENGS}
        self.lastc = {e: None for e in ENGS}
        self.lastw = {}
        self.readers = {}
        self.known = {e: {} for e in ENGS}
        self._ls_cv = None
        self._tls = threading.local()

    def _deps(self, reads, writes):
        deps = set()
        for k in reads:
            t = self.lastw.get(k)
            if t is not None:
                deps.add(t)
        for k in writes:
            t = self.lastw.get(k)
            if t is not None:
                deps.add(t)
            for r in self.readers.get(k, ()):
                deps.add(r)
        return deps

    def _filter(self, eng, deps):
        waits = []
        known = self.known[eng]
        for t in deps:
            if t[0] == "e":
                if t[1] == "pe" and eng == "pe":
                    continue
                key, val = ("e", t[1]), t[2]
            else:
                key, val = ("d", t[1]), t[2]
            if known.get(key, -1) >= val:
                continue
            known[key] = val
            waits.append(t)
            if t[0] == "e":
                self.ins[t[1]][t[2]]["inc"] = True
        return waits

    def _commit(self, tok, reads, writes):
        for k in writes:
            self.lastw[k] = tok
            self.readers[k] = []
        for k in reads:
            if k not in writes:
                self.readers.setdefault(k, []).append(tok)

    def op(self, eng, fn, reads, writes):
        reads, writes = _keys(reads), _keys(writes)
        waits = self._filter(eng, self._deps(reads, writes))
        idx = len(self.ins[eng])
        self.ins[eng].append(dict(fn=fn, waits=waits, inc=False, dma=None))
        self.lastc[eng] = idx
        self._commit(("e", eng, idx), reads, writes)
        self._yield()

    def dma(self, out, in_, eng="sp", fn=None, extra_reads=()):
        reads, writes = _keys([in_, *extra_reads]), _keys([out])
        if eng == "pool":
            s = self.n_hw + self.dnext_sw
            self.dnext_sw = (self.dnext_sw + 1) % (len(self.dsem) - self.n_hw)
        else:
            s = self.dnext
            self.dnext = (self.dnext + 1) % self.n_hw
        deps = self._deps(reads, writes)
        if self.dtgt[s] > 0:
            deps.add(("d", s, self.dtgt[s]))
        waits = self._filter(eng, deps)
        self.dtgt[s] += 16
        if fn is None:
            o, i = out.ap, in_.ap
            fn = lambda e: e.dma_start(out=o, in_=i)
        self.ins[eng].append(dict(fn=fn, waits=waits, inc=False, dma=s))
        self._commit(("d", s, self.dtgt[s]), reads, writes)
        self._yield()

    def _next_alive(self, i):
        n = len(self._ls_alive)
        for k in range(1, n + 1):
            j = (i + k) % n
            if self._ls_alive[j]:
                return j
        return -1

    def _yield(self):
        cv = self._ls_cv
        if cv is None:
            return
        i = self._tls.idx
        with cv:
            self._ls_turn = self._next_alive(i)
            cv.notify_all()
            while self._ls_turn != i:
                cv.wait()

    def lockstep(self, fns):
        cv = threading.Condition()
        self._ls_alive = [True] * len(fns)
        self._ls_turn = 0
        errs = []

        def run(i):
            self._tls.idx = i
            with cv:
                while self._ls_turn != i:
                    cv.wait()
            try:
                fns[i]()
            except BaseException as e:
                errs.append(e)
            with cv:
                self._ls_alive[i] = False
                self._ls_turn = self._next_alive(i)
                cv.notify_all()

        self._ls_cv = cv
        ths = [threading.Thread(target=run, args=(i,)) for i in range(len(fns))]
        for t in ths:
            t.start()
        for t in ths:
            t.join()
        self._ls_cv = None
        if errs:
            raise errs[0]

    def barrier(self):
        for e in ENGS:
            deps = set()
            for e2 in ENGS:
                if e2 != e and self.lastc[e2] is not None:
                    deps.add(("e", e2, self.lastc[e2]))
            for s, t in enumerate(self.dtgt):
                if t > 0:
                    deps.add(("d", s, t))
            waits = self._filter(e, deps)
            self.ins[e].append(dict(fn=None, waits=waits, inc=False, dma=None))
        self.lastw.clear()
        self.readers.clear()

    def emit(self):
        for e in ENGS:
            v = 0
            for i in self.ins[e]:
                if i["inc"]:
                    v += 1
                i["val"] = v
        sched = self

        def mk(e):
            def body(eng):
                if e == "pool":
                    sched.bnd_reg = eng.alloc_register("bnd")
                    eng.reg_mov(sched.bnd_reg, DEPTH * 16384 - 1)
                for i in sched.ins[e]:
                    for t in i["waits"]:
                        if t[0] == "e":
                            eng.wait_ge(sched.sem[t[1]], sched.ins[t[1]][t[2]]["val"])
                        else:
                            eng.wait_ge(sched.dsem[t[1]], t[2])
                    if i["fn"] is None:
                        continue
                    r = i["fn"](eng)
                    if i["dma"] is not None:
                        r.then_inc(sched.dsem[i["dma"]], 16)
                    elif i["inc"]:
                        r.then_inc(sched.sem[e], 1)
            return body

        with self.nc.Block() as block:
            block.tensor(mk("pe"))
            block.vector(mk("dve"))
            block.scalar(mk("act"))
            block.gpsimd(mk("pool"))
            block.sync(mk("sp"))

    def mm(self, out, lhsT, rhs, start=True, stop=True):
        o, a, b = out.ap, lhsT.ap, rhs.ap
        self.op("pe", lambda e: e.matmul(o, a, b, start=start, stop=stop), [lhsT, rhs], [out])

    def tr(self, out, in_, ident):
        o, a, b = out.ap, in_.ap, ident.ap
        self.op("pe", lambda e: e.transpose(o, a, b), [in_, ident], [out])

    def act(self, out, in_, func, bias=None, scale=1.0):
        o, a = out.ap, in_.ap
        kw = {}
        rd = [in_]
        if bias is not None:
            if isinstance(bias, T):
                kw["bias"] = bias.ap
                rd.append(bias)
            else:
                kw["bias"] = bias
        if isinstance(scale, T):
            rd.append(scale)
            sc = scale.ap
        else:
            sc = scale
        self.op("act", lambda e: e.activation(o, a, func, scale=sc, **kw), rd, [out])

    def tt(self, out, a, b, op, eng="dve"):
        o, x, y = out.ap, a.ap, b.ap
        self.op(eng, lambda e: e.tensor_tensor(o, x, y, op), [a, b], [out])

    def ts(self, out, a, s1, op0, s2=None, op1=None, eng="dve"):
        o, x = out.ap, a.ap
        rd = [a]
        v1 = s1
        if isinstance(s1, T):
            rd.append(s1)
            v1 = s1.ap
        v2 = s2
        if isinstance(s2, T):
            rd.append(s2)
            v2 = s2.ap
        if op1 is None:
            self.op(eng, lambda e: e.tensor_scalar(o, x, v1, None, op0), rd, [out])
        else:
            self.op(eng, lambda e: e.tensor_scalar(o, x, v1, v2, op0, op1), rd, [out])

    def stt(self, out, a, s, b, op0, op1, eng="dve"):
        o, x, y = out.ap, a.ap, b.ap
        rd = [a, b]
        v = s
        if isinstance(s, T):
            rd.append(s)
            v = s.ap
        self.op(eng, lambda e: e.scalar_tensor_tensor(o, x, v, y, op0, op1), rd, [out])

    def copy(self, out, in_, eng="dve"):
        o, a = out.ap, in_.ap
        if eng == "act":
            self.op("act", lambda e: e.copy(o, a), [in_], [out])
        else:
            self.op(eng, lambda e: e.tensor_copy(o, a), [in_], [out])

    def red(self, out, in_, op=ALU.add, axis=AX.X, eng="dve"):
        o, a = out.ap, in_.ap
        self.op(eng, lambda e: e.tensor_reduce(o, a, axis, op), [in_], [out])

    def recip(self, out, in_):
        o, a = out.ap, in_.ap
        self.op("dve", lambda e: e.reciprocal(o, a), [in_], [out])

    def rsqrt(self, out, in_):
        self.act(out, in_, AF.Sqrt)
        self.recip(out, out)

    def memset(self, out, val, eng="dve"):
        o = out.ap
        self.op(eng, lambda e: e.memset(o, val), [], [out])


class Arena:
    def __init__(self, nc, es, words, name="arena", rounded=False):
        self.rounded = rounded
        self.h = es.enter_context(nc.sbuf_tensor(name, [128, words], F32))
        self.words = words
        self.off = 0
        self.n = 0

    def alloc(self, w, name=None, parts=128):
        assert self.off + w <= self.words, f"arena overflow {self.off}+{w}>{self.words}"
        ap = self.h[0:parts, self.off:self.off + w]
        if self.rounded:
            ap = ap.bitcast(F32R)
        self.off += w
        self.n += 1
        return T(ap, name or f"buf{self.n}")

    def mark(self):
        return self.off

    def reset(self, to=0):
        self.off = to


class Ctx:
    pass


def layer_norm_tile(S_, x, g_bc, b_bc, st, tmp, width=D, eps=1e-5):
    S_.red(st[:, 0:1], x)
    S_.ts(st[:, 1:2], st[:, 0:1], -1.0 / width, ALU.mult)
    S_.act(x, x, AF.Identity, bias=st[:, 1:2])
    S_.tt(tmp, x, x, ALU.mult, eng="pool")
    S_.red(st[:, 2:3], tmp)
    S_.ts(st[:, 3:4], st[:, 2:3], 1.0 / width, ALU.mult, eps, ALU.add)
    S_.rsqrt(st[:, 3:4], st[:, 3:4])
    S_.stt(x, x, st[:, 3:4], g_bc, ALU.mult, ALU.mult)
    S_.tt(x, x, b_bc, ALU.add, eng="pool")


def sin_range(S_, A, out, ang, shift):
    n = ang.ap.shape[1]
    tmp = A.alloc(n)
    ki = A.alloc(n).cast(I32)
    kf = A.alloc(n)
    a2 = A.alloc(n)
    S_.ts(a2, ang, shift, ALU.add)
    S_.ts(tmp, a2, 1.0 / (2 * math.pi), ALU.mult)
    S_.copy(ki, tmp)
    S_.copy(kf, ki)
    S_.stt(tmp, kf, -2 * math.pi, a2, ALU.mult, ALU.add)
    S_.ts(kf, tmp, math.pi, ALU.is_ge)
    S_.stt(tmp, kf, -2 * math.pi, tmp, ALU.mult, ALU.add)
    S_.ts(tmp, tmp, math.pi, ALU.min, -math.pi, ALU.max)
    S_.act(out, tmp, AF.Sin)


def rope_apply(S_, x, a, r, cos, sin, tmps, H):
    x1 = x[:, :, a:a + r]
    x2 = x[:, :, a + r:a + 2 * r]
    c = cos.un(1).bc([128, H, r])
    s = sin.un(1).bc([128, H, r])
    t1, t2, t3, t4 = [t.re("p (h r) -> p h r", r=r) for t in tmps]
    S_.tt(t1, x1, c, ALU.mult)
    S_.tt(t2, x2, s, ALU.mult)
    S_.tt(t3, x2, c, ALU.mult, eng="pool")
    S_.tt(t4, x1, s, ALU.mult, eng="pool")
    S_.tt(x1, t1, t2, ALU.subtract)
    S_.tt(x2, t3, t4, ALU.add)


def rms_norm_tile(S_, x, g_bc, st, tmp, width, eps=1e-6):
    S_.tt(tmp, x, x, ALU.mult)
    S_.red(st[:, 0:1], tmp)
    S_.ts(st[:, 1:2], st[:, 0:1], 1.0 / width, ALU.mult, eps, ALU.add)
    S_.rsqrt(st[:, 1:2], st[:, 1:2])
    S_.stt(x, x, st[:, 1:2], g_bc, ALU.mult, ALU.mult)


def attention(C, l, qT_d, kT_d, v_d, o_d, dk, scale, biasT_d=None):
    S_, A, AR, ps, ident, triu = C.S_, C.A, C.AR, C.ps, C.ident, C.triu
    a0, r0 = A.mark(), AR.mark()
    qT = [AR.alloc(S, f"aqT{i}") for i in range(2)]
    kT = [AR.alloc(S, f"akT{i}") for i in range(2)]
    vv = [AR.alloc(NT * 66, f"av{i}").re("p (t e) -> p t e", e=66) for i in range(2)]
    pT = [AR.alloc(512, f"apT{i}") for i in range(3)]
    identr = AR.alloc(128, "identr")
    S_.dma(identr, C.ident_d.r)
    if biasT_d is not None:
        biasT = AR.alloc(S, "abiasT")
        S_.dma(biasT[0:64, :], biasT_d.r)
    ob = A.alloc(NT * 512, "aob").re("p (t c) -> p t c", c=512)
    rs = [A.alloc(1, f"ars{i}") for i in range(4)]
    def load_head(h):
        b = h % 2
        S_.dma(qT[b][0:dk, :], qT_d[h].r)
        S_.dma(kT[b][0:dk, :], kT_d[h].r)
        S_.copy(vv[b][:, :, 64:66], C.ones[:, 0:2].un(1).bc([128, NT, 2]), eng="pool")
        S_.dma(vv[b][:, :, 0:64], T(v_d.ap[:, h, :].rearrange("(t p) e -> p t e", p=128), v_d.key).r)

    steps = [(h, G, kt) for h in range(8) for G in range(4) for kt in range(4 * G + 4)]

    def emit_qk(i):
        h, G, kt = steps[i]
        b = h % 2
        sp = ps[i % 2]
        S_.mm(sp, kT[b][0:dk, kt * 128:(kt + 1) * 128], qT[b][0:dk, G * 512:(G + 1) * 512],
              start=True, stop=(biasT_d is None))
        if biasT_d is not None:
            hn = h * 8 + kt // 2
            S_.mm(sp, identr[0:64, hn:hn + 1].bc([64, 128]), biasT[0:64, G * 512:(G + 1) * 512],
                  start=False, stop=True)

    def emit_rest(i):
        h, G, kt = steps[i]
        b = h % 2
        if G == 0 and kt == 0 and h + 1 < 8:
            load_head(h + 1)
        sp = ps[i % 2]
        p = pT[i % 3]
        S_.act(p, sp, AF.Exp, scale=scale)
        for j in range(4):
            qt = 4 * G + j
            if qt < kt:
                continue
            if qt == kt:
                S_.tt(p[:, j * 128:(j + 1) * 128], p[:, j * 128:(j + 1) * 128], triu, ALU.mult, eng="pool")
            S_.mm(ps[2 + j][:, 0:66], p[:, j * 128:(j + 1) * 128], vv[b][:, kt, :], start=(kt == 0), stop=(kt == qt))
            if kt == qt:
                S_.recip(rs[j], ps[2 + j][:, 64:65])
                S_.ts(ob[:, qt, h * 64:(h + 1) * 64], ps[2 + j][:, 0:64], rs[j], ALU.mult)

    load_head(0)
    emit_qk(0)
    for i in range(len(steps)):
        if i + 1 < len(steps):
            emit_qk(i + 1)
        emit_rest(i)
    for t in range(NT):
        S_.dma(o_d[t * 128:(t + 1) * 128, :], ob[:, t, :])
    S_.barrier()
    A.reset(a0)
    AR.reset(r0)
def mla_prep(C, l):
    S_, A, AR, ps, ident = C.S_, C.A, C.AR, C.ps, C.ident
    a0, r0 = A.mark(), AR.mark()
    wuq = AR.alloc(6 * 768, "wuq").re("p (c n) -> p c n", c=6)
    S_.dma(wuq, T(C.mla_w_uq.ap[l].rearrange("(c p) n -> p c n", p=128), "mla_w_uq").r)
    wukv = AR.alloc(2 * 1024, "wukv").re("p (c n) -> p c n", c=2)
    S_.dma(wukv, T(C.mla_w_ukv.ap[l].rearrange("(c p) n -> p c n", p=128), "mla_w_ukv").r)
    gq = A.alloc(768, "gq")
    S_.dma(gq, T(C.mla_q_norm.ap[l].partition_broadcast(128), "mla_q_norm"))
    gkv = A.alloc(256, "gkv")
    S_.dma(gkv, T(C.mla_kv_norm.ap[l].partition_broadcast(128), "mla_kv_norm"))
    cin = [A.alloc(1056, f"cin{i}") for i in range(2)]
    tmp = A.alloc(768, "mtmp")
    st = A.alloc(4, "mst")
    cT = AR.alloc(8 * 128, "cT").re("p (c t) -> p c t", c=8)
    qtm = A.alloc(768, "qtm")
    ktm = A.alloc(768, "ktm").re("p (h d) -> p h d", d=96)
    vtm = [A.alloc(512, f"vtm{i}").re("p (h e) -> p h e", e=64) for i in range(2)]
    rt = [A.alloc(8 * 16, f"rt{i}") for i in range(4)]
    qTs = [A.alloc(8 * 128, f"qTs{i}").re("p (h t) -> p h t", h=8) for i in range(2)]
    kTs = [A.alloc(8 * 128, f"kTs{i}").re("p (h t) -> p h t", h=8) for i in range(2)]
    for t in range(NT):
        b = t % 2
        ci = cin[b]
        S_.dma(ci, C.proj[t * 128:(t + 1) * 128, 0:1056].k((t, 0)))
        rms_norm_tile(S_, ci[:, 0:768], gq, st, tmp, 768)
        rms_norm_tile(S_, ci[:, 768:1024], gkv, st, tmp[:, 0:256], 256)
        for c in range(8):
            p = ps[6 + c % 2]
            S_.tr(p[:, 0:128], ci[:, c * 128:(c + 1) * 128], ident)
            S_.copy(cT[:, c, :], p[:, 0:128], eng=("dve" if c % 2 == 0 else "act"))
        for c in range(6):
            S_.mm(ps[0], cT[:, c, :], wuq[:, c, 0:512], start=(c == 0), stop=(c == 5))
        for c in range(6):
            S_.mm(ps[1][:, 0:256], cT[:, c, :], wuq[:, c, 512:768], start=(c == 0), stop=(c == 5))
        for c in range(2):
            S_.mm(ps[2], cT[:, 6 + c, :], wukv[:, c, 0:512], start=(c == 0), stop=(c == 1))
        for c in range(2):
            S_.mm(ps[3], cT[:, 6 + c, :], wukv[:, c, 512:1024], start=(c == 0), stop=(c == 1))
        S_.copy(qtm[:, 0:512], ps[0])
        S_.copy(qtm[:, 512:768], ps[1][:, 0:256], eng="act")
        q3 = qtm.re("p (h d) -> p h d", d=96)
        rope_apply(S_, q3, 64, 16, C.cosA[:, t, :], C.sinA[:, t, :], rt, 8)
        kr = ci[:, 1024:1056].re("p (h d) -> p h d", h=1)
        rope_apply(S_, kr, 0, 16, C.cosA[:, t, :], C.sinA[:, t, :], [x[:, 0:16] for x in rt], 1)
        for half in range(2):
            kv = ps[2 + half].re("p (h e) -> p h e", e=128)
            S_.copy(ktm[:, half * 4:(half + 1) * 4, 0:64], kv[:, :, 0:64])
            S_.copy(vtm[b][:, half * 4:(half + 1) * 4, :], kv[:, :, 64:128], eng="act")
        S_.copy(ktm[:, :, 64:96], kr.bc([128, 8, 32]), eng="pool")
        S_.dma(T(C.va.ap[t * 128:(t + 1) * 128], "va").k(t), vtm[b])
        for src, dst, dstd in ((q3, qTs[b], C.qTa), (ktm, kTs[b], C.kTa)):
            for h in range(8):
                p = ps[4 + h // 4]
                S_.tr(p[0:96, (h % 4) * 128:(h % 4 + 1) * 128], src[:, h, :], ident)
            S_.copy(dst[0:96, 0:4, :], ps[4][0:96, :].re("p (h t) -> p h t", h=4))
            S_.copy(dst[0:96, 4:8, :], ps[5][0:96, :].re("p (h t) -> p h t", h=4), eng="act")
            S_.dma(T(dstd.ap[:, :, t * 128:(t + 1) * 128].rearrange("h d t -> d h t"), dstd.key).k(t), dst[0:96, :, :])
    S_.barrier()
    A.reset(a0)
    AR.reset(r0)


def moba_prep(C, l):
    S_, A, AR, ps, ident = C.S_, C.A, C.AR, C.ps, C.ident
    a0, r0 = A.mark(), AR.mark()
    m = [A.alloc(1536, f"mb{i}") for i in range(2)]
    rt = [A.alloc(8 * 8, f"brt{i}") for i in range(4)]
    qTs = [A.alloc(8 * 128, f"bqTs{i}").re("p (h t) -> p h t", h=8) for i in range(2)]
    kTs = [A.alloc(8 * 128, f"bkTs{i}").re("p (h t) -> p h t", h=8) for i in range(2)]
    qTr = AR.alloc(8 * 128, "bqTr").re("p (h t) -> p h t", h=8)
    kmT = A.alloc(8 * 16, "kmT").re("p (h t) -> p h t", h=8)
    kmean = AR.alloc(64, "kmean").re("p (h n) -> p h n", h=8)
    S_.copy(kmean, C.zeros[:, 0:64].re("p (h n) -> p h n", h=8))
    gm = A.alloc(64, "gm").re("p (h n) -> p h n", h=8)
    top = A.alloc(64, "gtop").re("p (h n) -> p h n", h=8)
    bias = [A.alloc(64, f"gbias{i}").re("p (h n) -> p h n", h=8) for i in range(2)]
    bTs = [A.alloc(128, f"bTs{i}") for i in range(2)]
    for t in range(NT):
        b = t % 2
        own = t // 2
        mt = m[b]
        S_.dma(mt, C.proj[t * 128:(t + 1) * 128, O_MOBA:O_MOBA + 1536].k((t, 1)))
        q3 = mt[:, 0:512].re("p (h d) -> p h d", d=64)
        k3 = mt[:, 512:1024].re("p (h d) -> p h d", d=64)
        rope_apply(S_, q3, 0, 8, C.cosB[:, t, :], C.sinB[:, t, :], rt, 8)
        rope_apply(S_, k3, 0, 8, C.cosB[:, t, :], C.sinB[:, t, :], rt, 8)
        S_.dma(T(C.vb.ap[t * 128:(t + 1) * 128], "vb").k(t), mt[:, 1024:1536].re("p (h e) -> p h e", e=64))
        for src, dst, dstd in ((q3, qTs[b], C.qTb), (k3, kTs[b], C.kTb)):
            for h in range(8):
                p = ps[4 + h // 4]
                S_.tr(p[0:64, (h % 4) * 128:(h % 4 + 1) * 128], src[:, h, :], ident)
            S_.copy(dst[0:64, 0:4, :], ps[4][0:64, :].re("p (h t) -> p h t", h=4))
            S_.copy(dst[0:64, 4:8, :], ps[5][0:64, :].re("p (h t) -> p h t", h=4), eng="act")
            S_.dma(T(dstd.ap[:, :, t * 128:(t + 1) * 128].rearrange("h d t -> d h t"), dstd.key).k(t), dst[0:64, :, :])
        S_.red(kmT[0:64, :, t], kTs[b][0:64, :, :])
        bi = bias[b]
        if own >= 1:
            S_.copy(qTr[0:64], qTs[b][0:64], eng="pool")
            for h in range(8):
                S_.mm(ps[0][:, h * 8:(h + 1) * 8], qTr[0:64, h, :], kmean[0:64, h, :])
            S_.copy(gm, ps[0][:, 0:64].re("p (h n) -> p h n", h=8))
            S_.memset(gm[:, :, own:8], -1e30)
            for h in range(8):
                o_, i_ = top[:, h, :].ap, gm[:, h, :].ap
                S_.op("dve", (lambda o_, i_: (lambda e: e.max(out=o_, in_=i_)))(o_, i_), [gm], [top])
            S_.tt(bi, gm, top[:, :, 2:3].bc([128, 8, 8]), ALU.is_ge)
            S_.ts(bi, bi, -1.0, ALU.add, -NEG, ALU.mult)
        S_.memset(bi[:, :, own:own + 1], 0.0)
        if own < 7:
            S_.memset(bi[:, :, own + 1:8], NEG)
        S_.tr(ps[1][0:64, 0:128], bi.re("p h n -> p (h n)"), ident)
        S_.copy(bTs[b][0:64, :], ps[1][0:64, 0:128])
        S_.dma(C.biasTb[:, t * 128:(t + 1) * 128].k(t), bTs[b][0:64, :])
        if t % 2 == 1:
            n = t // 2
            S_.tt(kmean[0:64, :, n], kmT[0:64, :, t - 1], kmT[0:64, :, t], ALU.add)
            S_.ts(kmean[0:64, :, n], kmean[0:64, :, n], 1.0 / 256.0, ALU.mult)
    S_.barrier()
    A.reset(a0)
    AR.reset(r0)
def gdn_stage(C, l):
    S_, A, AR, ps, ident = C.S_, C.A, C.AR, C.ps, C.ident
    a0 = A.mark()
    P64 = 64
    i64 = ident[0:64, 0:64]

    def al(w, name):
        return A.alloc(w, name)[0:64, :]

    def hv(t):
        return t.re("p (h e) -> p h e", h=8)

    wconv_f = al(4 * 1536, "wconv")
    S_.dma(wconv_f, T(C.gdn_conv_w.ap[l].partition_broadcast(64), "gdn_conv_w"))
    wconv = wconv_f.re("p (j c) -> p j c", j=4)
    alog = al(8, "alog"); dtb = al(8, "dtb"); gno = al(64, "gno")
    S_.dma(alog, T(C.gdn_A_log.ap[l].partition_broadcast(64), "gdn_A_log"))
    S_.dma(dtb, T(C.gdn_dt_bias.ap[l].partition_broadcast(64), "gdn_dt_bias"))
    S_.dma(gno, T(C.gdn_o_norm.ap[l].partition_broadcast(64), "gdn_o_norm"))
    negA = al(8, "negA")
    S_.act(negA, alog, AF.Exp)
    S_.ts(negA, negA, -1.0, ALU.mult)
    ones = al(64, "ones64"); S_.memset(ones, 1.0)
    tri = al(64, "tri64"); S_.dma(tri, C.tri64_d)
    trilS = al(64, "trilS"); S_.dma(trilS, C.trilS_d)
    triuS = al(64, "triuS"); S_.dma(triuS, C.triuS_d)
    St = al(512, "gstate"); S_.memset(St, 0.0)
    xs2 = [al(1536, f"xs{j}") for j in range(2)]
    xs1 = [xs2[0], xs2[1], xs2[0], xs2[1]]
    xs = [xs1, xs1]
    y = al(1536, "gy"); ytmp = al(1536, "gytmp")
    zt1 = al(512, "gz")
    zt = [zt1, zt1]
    ab = [al(16, f"gab{b}") for b in range(2)]
    sm = al(64, "gsm")
    g_, beta, gcum, eg, egl, edec, rq = [sm[:, i * 8:(i + 1) * 8] for i in range(7)]
    ss = al(16, "gss")
    kb = al(512, "gkb"); vbt = al(512, "gvb"); kbg = al(512, "gkbg"); kdec = al(512, "gkdec")
    kT = al(512, "gkT"); qT = al(512, "gqT")
    dg = ytmp[:, 512:1024]; db = ytmp[:, 1024:1536]
    dd = al(512, "gdd"); dt_ = al(512, "gdt"); dtI = al(512, "gdtI")
    Pb = [al(512, f"gP{i}") for i in range(2)]
    Qb = [al(512, f"gQ{i}") for i in range(2)]
    R = al(512, "gR")
    u = al(512, "gu"); wT = al(512, "gwT"); qkT = al(512, "gqkT"); vnew = al(512, "gvnew")
    o = al(512, "go"); osq = ytmp[:, 0:512]; ot1 = al(512, "got"); ot = [ot1, ot1]
    p64 = [T(p.ap[0:64, :], p.key) for p in ps]

    def bc8(s):
        return s.un(2).bc([64, 8, 64])

    def bcm(mt):
        return mt.un(1).bc([64, 8, 64])

    for ci in range(min(S // 64, GDN_CHUNKS)):
        b = ci % 2
        c0 = ci * 64
        S_.dma(zt[b], C.proj[c0:c0 + 64, O_GZ:O_GZ + 512])
        S_.dma(ab[b], C.proj[c0:c0 + 64, O_GA:O_GA + 16])
        for j in range(4):
            sh = 3 - j
            xt = xs[b][j]
            if c0 - sh < 0:
                S_.memset(xt, 0.0)
                S_.dma(xt[sh:64, :], C.proj[0:64 - sh, O_GQKV:O_GQKV + 1536])
            else:
                S_.dma(xt, C.proj[c0 - sh:c0 - sh + 64, O_GQKV:O_GQKV + 1536])
            if j == 0:
                S_.tt(y, xt, wconv[:, 0, :], ALU.mult)
            else:
                S_.tt(ytmp, xt, wconv[:, j, :], ALU.mult, eng="pool")
                S_.tt(y, y, ytmp, ALU.add)
        S_.act(y, y, AF.Silu)
        if GDN_CUT == 1:
            continue
        S_.tt(ytmp[:, 0:1024], y[:, 0:1024], y[:, 0:1024], ALU.mult, eng="pool")
        S_.red(ss, ytmp[:, 0:1024].re("p (h e) -> p h e", e=64))
        S_.ts(ss, ss, 1e-6, ALU.add)
        S_.rsqrt(ss, ss)
        S_.ts(ss[:, 0:8], ss[:, 0:8], 0.125, ALU.mult)
        S_.tt(y[:, 0:1024].re("p (h e) -> p h e", e=64), y[:, 0:1024].re("p (h e) -> p h e", e=64),
              ss.un(2).bc([64, 16, 64]), ALU.mult)
        q3, k3, v3 = hv(y[:, 0:512]), hv(y[:, 512:1024]), hv(y[:, 1024:1536])
        S_.act(beta, ab[b][:, 8:16], AF.Sigmoid)
        S_.tt(g_, ab[b][:, 0:8], dtb, ALU.add)
        S_.act(g_, g_, AF.Exp)
        S_.act(g_, g_, AF.Ln, bias=1.0)
        S_.tt(g_, g_, negA, ALU.mult)
        S_.mm(p64[0][:, 0:8], tri, g_)
        S_.mm(p64[0][:, 8:16], ones, g_)
        S_.copy(gcum, p64[0][:, 0:8])
        S_.act(eg, p64[0][:, 0:8], AF.Exp)
        S_.act(egl, p64[0][:, 8:16], AF.Exp)
        S_.tt(edec, p64[0][:, 8:16], gcum, ALU.subtract)
        S_.act(edec, edec, AF.Exp)
        if GDN_CUT == 2:
            continue
        S_.tt(hv(kb), k3, bc8(beta), ALU.mult)
        S_.tt(hv(vbt), v3, bc8(beta), ALU.mult, eng="pool")
        S_.tt(hv(kbg), hv(kb), bc8(eg), ALU.mult)
        S_.tt(hv(kdec), k3, bc8(edec), ALU.mult, eng="pool")
        for h in range(8):
            S_.mm(p64[6][:, h * 64:(h + 1) * 64], k3[:, h, :], i64)
        for h in range(8):
            S_.mm(p64[7][:, h * 64:(h + 1) * 64], q3[:, h, :], i64)
        S_.copy(kT, p64[6])
        S_.copy(qT, p64[7], eng="act")
        if GDN_CUT == 3:
            continue
        S_.tt(hv(dg), bcm(i64), bc8(gcum), ALU.mult)
        S_.tt(hv(db), bcm(i64), bc8(beta), ALU.mult, eng="pool")
        S_.mm(p64[1], ones, dg)
        S_.mm(p64[2], ones, db)
        S_.tt(hv(dd), bc8(gcum), hv(p64[1]), ALU.subtract)
        S_.ts(dd, dd, 0.0, ALU.min)
        S_.act(dd, dd, AF.Exp)
        S_.tt(hv(dd), hv(dd), bcm(trilS), ALU.mult)
        S_.tt(hv(dt_), hv(p64[1]), bc8(gcum), ALU.subtract)
        S_.ts(dt_, dt_, 0.0, ALU.min)
        S_.act(dt_, dt_, AF.Exp)
        S_.tt(hv(dtI), hv(dt_), bcm(tri), ALU.mult, eng="pool")
        S_.tt(hv(dt_), hv(dt_), bcm(triuS), ALU.mult)
        if GDN_CUT == 4:
            continue
        for h in range(8):
            S_.mm(p64[3][:, h * 64:(h + 1) * 64], hv(kT)[:, h, :], hv(kT)[:, h, :])
        P, Q = Pb[0], Qb[0]
        S_.stt(P, p64[3], -1.0, dt_, ALU.mult, ALU.mult)
        S_.tt(P, P, p64[2], ALU.mult)
        S_.stt(Q, p64[3], -1.0, dd, ALU.mult, ALU.mult)
        S_.tt(hv(Q), hv(Q), bc8(beta), ALU.mult)
        S_.tt(hv(R), hv(P), bcm(i64), ALU.add)
        if GDN_CUT == 5:
            continue
        for lv in range(1, 6):
            Pn, Qn = Pb[lv % 2], Qb[lv % 2]
            if lv < 5:
                for h in range(8):
                    S_.mm(p64[4][:, h * 64:(h + 1) * 64], hv(Q)[:, h, :], hv(P)[:, h, :])
            for h in range(8):
                S_.mm(p64[5][:, h * 64:(h + 1) * 64], hv(P)[:, h, :], hv(Q)[:, h, :])
            if lv < 5:
                S_.copy(Pn, p64[4])
            S_.copy(Qn, p64[5], eng="act")
            for h in range(8):
                S_.mm(p64[0][:, h * 64:(h + 1) * 64], hv(Qn)[:, h, :], hv(R)[:, h, :])
            S_.tt(R, R, p64[0], ALU.add)
            P, Q = Pn, Qn
        if GDN_CUT == 6:
            continue
        for h in range(8):
            S_.mm(p64[1][:, h * 64:(h + 1) * 64], hv(R)[:, h, :], hv(vbt)[:, h, :])
        for h in range(8):
            S_.mm(p64[2][:, h * 64:(h + 1) * 64], hv(kbg)[:, h, :], hv(R)[:, h, :])
        for h in range(8):
            S_.mm(p64[3][:, h * 64:(h + 1) * 64], hv(kT)[:, h, :], hv(qT)[:, h, :])
        S_.copy(u, p64[1])
        S_.copy(wT, p64[2], eng="act")
        S_.tt(qkT, p64[3], dtI, ALU.mult)
        if GDN_CUT == 7:
            continue
        for h in range(8):
            S_.mm(p64[4][:, h * 64:(h + 1) * 64], hv(wT)[:, h, :], hv(St)[:, h, :])
        for h in range(8):
            S_.mm(p64[5][:, h * 64:(h + 1) * 64], hv(qT)[:, h, :], hv(St)[:, h, :])
        S_.tt(vnew, u, p64[4], ALU.subtract)
        for h in range(8):
            S_.mm(p64[6][:, h * 64:(h + 1) * 64], hv(qkT)[:, h, :], hv(vnew)[:, h, :])
        for h in range(8):
            S_.mm(p64[7][:, h * 64:(h + 1) * 64], hv(kdec)[:, h, :], hv(vnew)[:, h, :])
        S_.tt(hv(o), hv(p64[5]), bc8(eg), ALU.mult)
        S_.tt(o, o, p64[6], ALU.add)
        S_.tt(hv(St), hv(St), bc8(egl), ALU.mult)
        S_.tt(St, St, p64[7], ALU.add)
        if GDN_CUT == 8:
            continue
        S_.tt(osq, o, o, ALU.mult, eng="pool")
        S_.red(rq, hv(osq))
        S_.ts(rq, rq, 1.0 / 64, ALU.mult, 1e-6, ALU.add)
        S_.rsqrt(rq, rq)
        S_.tt(hv(o), hv(o), bc8(rq), ALU.mult)
        S_.tt(hv(o), hv(o), gno.un(1).bc([64, 8, 64]), ALU.mult)
        S_.act(zt[b], zt[b], AF.Silu)
        S_.tt(ot[b], o, zt[b], ALU.mult)
        S_.dma(C.oc[c0:c0 + 64, :].k(ci), ot[b])
    S_.barrier()
    A.reset(a0)
def merge_stage(C, l):
    S_, A, AR, ps, ident = C.S_, C.A, C.AR, C.ps, C.ident
    a0, r0 = A.mark(), AR.mark()
    wbr = AR.alloc(12 * 1024, "wbr").re("p (n c d) -> p n c d", n=3, c=4)
    for n in range(3):
        S_.dma(wbr[:, n], T(C.w_branch.ap[l, n].rearrange("(c p) d -> p c d", p=128), "w_branch").r)
    wout = AR.alloc(8 * 1024, "wout").re("p (c d) -> p c d", c=8)
    S_.dma(wout, T(C.w_out.ap[l].rearrange("(c p) d -> p c d", p=128), "w_out").r)
    oT = AR.alloc(12 * 128, "oT").re("p (c t) -> p c t", c=12)
    mT = AR.alloc(8 * 128, "mT").re("p (c t) -> p c t", c=8)
    gb = A.alloc(3072, "gb"); S_.dma(gb, T(C.gate_bias.ap[l].partition_broadcast(128), "gate_bias"))
    g1 = A.alloc(D, "g1"); S_.dma(g1, T(C.ln1_g.ap[l].partition_broadcast(128), "ln1_g"))
    b1 = A.alloc(D, "b1"); S_.dma(b1, T(C.ln1_b.ap[l].partition_broadcast(128), "ln1_b"))
    o3 = A.alloc(1536, "o3"); gl = A.alloc(3072, "gl"); mg = A.alloc(D, "mg"); tmp = A.alloc(D, "mtmp2")
    hh = A.alloc(D, "hh"); st = A.alloc(4, "mst2")
    for t in range(min(NT, NT_LIMIT)):
        r0_, r1_ = t * 128, (t + 1) * 128
        S_.dma(o3[:, 0:512], C.oa[r0_:r1_, :])
        S_.dma(o3[:, 512:1024], C.ob[r0_:r1_, :])
        S_.dma(o3[:, 1024:1536], C.oc[r0_:r1_, :])
        S_.dma(gl, C.proj[r0_:r1_, O_GATE:O_GATE + 3072])
        S_.dma(hh, C.hbuf[r0_:r1_, :].k(t))
        for c in range(12):
            p = ps[6 + c % 2]
            S_.tr(p[:, 0:128], o3[:, c * 128:(c + 1) * 128], ident)
            S_.copy(oT[:, c, :], p[:, 0:128], eng=("dve" if c % 2 == 0 else "act"))
        S_.tt(gl, gl, gb, ALU.add, eng="pool")
        S_.act(gl, gl, AF.Sigmoid)
        for n in range(3):
            for half in range(2):
                for c in range(4):
                    S_.mm(ps[n * 2 + half], oT[:, n * 4 + c, :], wbr[:, n, c, half * 512:(half + 1) * 512],
                          start=(c == 0), stop=(c == 3))
        for half in range(2):
            hs = slice(half * 512, (half + 1) * 512)
            S_.tt(mg[:, hs], gl[:, half * 512:(half + 1) * 512], ps[half], ALU.mult)
            for n in (1, 2):
                S_.tt(tmp[:, hs], gl[:, n * 1024 + half * 512:n * 1024 + (half + 1) * 512], ps[n * 2 + half], ALU.mult)
                S_.tt(mg[:, hs], mg[:, hs], tmp[:, hs], ALU.add, eng="pool")
        for c in range(8):
            p = ps[6 + c % 2]
            S_.tr(p[:, 0:128], mg[:, c * 128:(c + 1) * 128], ident)
            S_.copy(mT[:, c, :], p[:, 0:128], eng=("dve" if c % 2 == 0 else "act"))
        for half in range(2):
            for c in range(8):
                S_.mm(ps[half], mT[:, c, :], wout[:, c, half * 512:(half + 1) * 512], start=(c == 0), stop=(c == 7))
        for half in range(2):
            hs = slice(half * 512, (half + 1) * 512)
            S_.stt(hh[:, hs], hh[:, hs], ALPHA, ps[half], ALU.mult, ALU.add)
        layer_norm_tile(S_, hh, g1, b1, st, tmp)
        S_.dma(C.hbuf[r0_:r1_, :].k(t), hh)
    S_.barrier()
    A.reset(a0)
    AR.reset(r0)


def peer_stage(C, l):
    S_, A, AR, ps, ident = C.S_, C.A, C.AR, C.ps, C.ident
    a0, r0 = A.mark(), AR.mark()
    NTL = min(NT, NT_LIMIT)
    hT = AR.alloc(8 * S, "phT").re("p (c t) -> p c t", c=8)
    ht = [A.alloc(D, f"pht{i}") for i in range(2)]
    for t in range(NTL):
        b = t % 2
        S_.dma(ht[b], C.hbuf[t * 128:(t + 1) * 128, :].k(t))
        for c in range(8):
            p = ps[6 + c % 2]
            S_.tr(p[:, 0:128], ht[b][:, c * 128:(c + 1) * 128], ident)
            S_.copy(hT[:, c, t * 128:(t + 1) * 128], p[:, 0:128], eng=("dve" if c % 2 == 0 else "act"))
    wq = [AR.alloc(8 * 128, f"pwq{i}").re("p (c n) -> p c n", c=8) for i in range(2)]
    sk = [AR.alloc(128, f"psk{i}") for i in range(2)]
    qT = [AR.alloc(S, f"pqT{i}") for i in range(2)]
    sct = [A.alloc(512, f"psct{i}") for i in range(2)]
    wq_l = T(C.peer_w_q.ap[l].rearrange("(c p) n -> p c n", p=128), "peer_w_q")
    NG = (NTL + 3) // 4
    for hp in range(16):
        b = hp % 2
        S_.dma(wq[b], wq_l[:, :, hp * 128:(hp + 1) * 128].r)
        S_.dma(sk[b], T(C.peer_skT.ap[l, hp], "peer_skT").r)
        for g in range(NG):
            p = ps[g % 2]
            for c in range(8):
                S_.mm(p, wq[b][:, c, :], hT[:, c, g * 512:(g + 1) * 512], start=(c == 0), stop=(c == 7))
            S_.copy(qT[b][:, g * 512:(g + 1) * 512], p, eng=("dve" if g % 2 == 0 else "act"))
        for g in range(NG):
            p = ps[2 + g % 2]
            for j in range(4):
                tt_ = g * 4 + j
                S_.mm(p[:, j * 128:(j + 1) * 128], qT[b][:, tt_ * 128:(tt_ + 1) * 128], sk[b])
            S_.copy(sct[g % 2], p, eng=("dve" if g % 2 == 0 else "act"))
            S_.dma(T(C.sc.ap[g * 512:(g + 1) * 512, hp, :].rearrange("(j p) n -> p j n", p=128), "sc").k((g, hp)),
                   sct[g % 2].re("p (j n) -> p j n", j=4))
    S_.barrier()
    A.reset(a0)
    AR.reset(r0)
    g2 = A.alloc(D, "g2"); S_.dma(g2, T(C.ln2_g.ap[l].partition_broadcast(128), "ln2_g"))
    b2 = A.alloc(D, "b2"); S_.dma(b2, T(C.ln2_b.ap[l].partition_broadcast(128), "ln2_b"))
    sc3 = A.alloc(2048, "sc3")
    wk = A.alloc(128, "pwk")
    stop_ = A.alloc(256, "pstop").re("p (h k) -> p h k", k=16)
    itop = A.alloc(256, "pitop").cast(U32).re("p (h k) -> p h k", k=16)
    itopf = A.alloc(256, "pitopf")
    cand = A.alloc(256, "pcand"); cwk = A.alloc(256, "pcwk")
    cs = A.alloc(128, "pcs").re("p (h k) -> p h k", k=16)
    cpos = A.alloc(128, "pcpos").cast(U32).re("p (h k) -> p h k", k=16)
    ai = A.alloc(128, "pai").cast(I32); bi = A.alloc(128, "pbi").cast(I32)
    af = A.alloc(128, "paf"); bf = A.alloc(128, "pbf")
    eq = A.alloc(2048, "peq")
    i1s = A.alloc(128, "pi1s"); i2s = A.alloc(128, "pi2s"); idf = A.alloc(128, "pidf")
    idx2 = [A.alloc(128, f"pidx{i}").cast(I32) for i in range(2)]
    ee = A.alloc(128, "pee"); se = A.alloc(8, "pse")
    gate2 = [A.alloc(128, f"pgate{i}") for i in range(2)]
    actv = A.alloc(128, "pactv"); coef = A.alloc(128, "pcoef")
    xt2 = [A.alloc(D, f"pxt{i}") for i in range(2)]
    acc = A.alloc(D, "pacc"); accb = A.alloc(D, "paccb"); tmp = A.alloc(D, "ptmp"); st = A.alloc(4, "pst")
    NB = 8
    gbuf = [A.alloc(D, f"pg{i}") for i in range(NB)]
    junk = [A.alloc(D, f"pj{i}") for i in range(2)]
    nrows = DEPTH * 16384

    def vmax(o, i):
        oa, ia = o.ap, i.ap
        S_.op("dve", lambda e: e.max(out=oa, in_=ia), [i], [o])

    def vmaxidx(o, m, i):
        oa, ma, ia = o.ap, m.ap, i.ap
        S_.op("dve", lambda e: e.max_index(oa, ma, ia), [m, i], [o])

    def vrepl(o, m, i):
        oa, ma, ia = o.ap, m.ap, i.ap
        S_.op("dve", lambda e: e.match_replace(out=oa, in_to_replace=ma, in_values=ia, imm_value=-1e30), [m, i], [o])

    def gather(dst, table, s, idx):
        o_, t_, i_ = dst.ap, table.ap, idx[:, s:s + 1].ap
        S_.dma(dst, table, eng="pool", extra_reads=[idx],
               fn=lambda e: e.indirect_dma_start(out=o_, out_offset=None, in_=t_,
                                                 in_offset=bass.IndirectOffsetOnAxis(ap=i_, axis=0),
                                                 bounds_check=S_.bnd_reg, oob_is_err=False))

    def select(t):
        r0_, r1_ = t * 128, (t + 1) * 128
        idx, gate, xt = idx2[t % 2], gate2[t % 2], xt2[t % 2]
        S_.dma(sc3, T(C.sc.ap[r0_:r1_].rearrange("p h n -> p (h n)"), "sc"))
        S_.dma(xt, C.hbuf[r0_:r1_, :].k(t))
        for hp in range(16):
            v = sc3[:, hp * 128:(hp + 1) * 128]
            vmax(stop_[:, hp, 0:8], v)
            vmaxidx(itop[:, hp, 0:8], stop_[:, hp, 0:8], v)
            vrepl(wk, stop_[:, hp, 0:8], v)
            vmax(stop_[:, hp, 8:16], wk)
            vmaxidx(itop[:, hp, 8:16], stop_[:, hp, 8:16], wk)
        S_.copy(itopf, itop.re("p h k -> p (h k)"))
        s4 = stop_.re("p (h q) k -> p h q k", q=2)
        c3 = cand.re("p (a b) -> p a b", b=16)
        for h in range(8):
            S_.tt(c3, s4[:, h, 0, :].un(2).bc([128, 16, 16]), s4[:, h, 1, :].un(1).bc([128, 16, 16]), ALU.add)
            vmax(cs[:, h, 0:8], cand)
            vmaxidx(cpos[:, h, 0:8], cs[:, h, 0:8], cand)
            vrepl(cwk, cs[:, h, 0:8], cand)
            vmax(cs[:, h, 8:16], cwk)
            vmaxidx(cpos[:, h, 8:16], cs[:, h, 8:16], cwk)
        cpf = cpos.re("p h k -> p (h k)").cast(I32)
        ca, cpa = ai.ap, cpf.ap
        S_.op("dve", lambda e: e.tensor_single_scalar(ca, cpa, 4, ALU.logical_shift_right), [cpos], [ai])
        cb = bi.ap
        S_.op("dve", lambda e: e.tensor_single_scalar(cb, cpa, 15, ALU.bitwise_and), [cpos], [bi])
        S_.copy(af, ai)
        S_.copy(bf, bi)
        i4 = itopf.re("p (h q k) -> p h q k", q=2, k=16)
        eq4 = eq.re("p (h k a) -> p h k a", h=8, k=16)
        iota_bc = C.iota16.un(1).un(1).bc([128, 8, 16, 16])
        for src, q, dst in ((af, 0, i1s), (bf, 1, i2s)):
            S_.tt(eq4, iota_bc, src.re("p (h k) -> p h k", k=16).un(3).bc([128, 8, 16, 16]), ALU.is_equal)
            S_.tt(eq4, eq4, i4[:, :, q, :].un(2).bc([128, 8, 16, 16]), ALU.mult)
            S_.red(dst, eq.re("p (s a) -> p s a", a=16))
        S_.stt(idf, i1s, 128.0, i2s, ALU.mult, ALU.add)
        if l > 0:
            S_.ts(idf, idf, float(l * 16384), ALU.add)
        S_.copy(idx, idf)
        cs2 = cs
        S_.tt(ee.re("p (h k) -> p h k", k=16), cs2, cs2[:, :, 0:1].bc([128, 8, 16]), ALU.subtract)
        S_.act(ee, ee, AF.Exp)
        S_.red(se, ee.re("p (h k) -> p h k", k=16))
        S_.recip(se, se)
        S_.tt(gate.re("p (h k) -> p h k", k=16), ee.re("p (h k) -> p h k", k=16), se.un(2).bc([128, 8, 16]), ALU.mult)
    def experts(t):
        r0_, r1_ = t * 128, (t + 1) * 128
        idx, gate, xt = idx2[t % 2], gate2[t % 2], xt2[t % 2]
        for s in range(128 + NB - 1):
            if s < 128:
                gather(gbuf[s % NB], C.peer_u, s, idx)
            s2 = s - (NB - 1)
            if s2 >= 0:
                j_, x_, u_, a_ = junk[s2 % 2].ap, xt.ap, gbuf[s2 % NB].ap, actv[:, s2:s2 + 1].ap
                S_.op("dve", (lambda j_, x_, u_, a_: (lambda e: e.scalar_tensor_tensor(j_, x_, 1.0, u_, ALU.mult, ALU.mult, accum_out=a_)))(j_, x_, u_, a_),
                      [xt, gbuf[s2 % NB]], [junk[s2 % 2], actv.k(s2)])
        co_, av_ = coef.ap, actv.ap
        S_.op("act", (lambda co_, av_: (lambda e: e.activation(co_, av_, AF.Gelu)))(co_, av_),
              [actv.k(s_) for s_ in range(128)], [coef])
        S_.tt(coef, coef, gate, ALU.mult)
        S_.memset(acc, 0.0)
        S_.memset(accb, 0.0, eng="pool")
        for s in range(128 + NB - 1):
            if s < 128:
                gather(gbuf[s % NB], C.peer_v, s, idx)
            s2 = s - (NB - 1)
            if s2 >= 0:
                ac_ = acc if s2 % 2 == 0 else accb
                S_.stt(ac_, gbuf[s2 % NB], coef[:, s2:s2 + 1], ac_, ALU.mult, ALU.add)
        S_.tt(acc, acc, accb, ALU.add)
        S_.stt(xt, xt, ALPHA, acc, ALU.mult, ALU.add)
        layer_norm_tile(S_, xt, g2, b2, st, tmp)
        S_.dma(C.hbuf[r0_:r1_, :].k(t), xt)
    select(0)
    for t in range(NTL):
        if t + 1 < NTL:
            S_.lockstep([(lambda t=t: experts(t)), (lambda t=t: select(t + 1))])
        else:
            experts(t)
    S_.barrier()
    A.reset(a0)
    AR.reset(r0)


def build(debug_outs=(), stages=None, depth=DEPTH):
    nc = bass.Bass("TRN2", target_bir_lowering=False)
    nc.dge_precook = False
    es = ExitStack()
    C = Ctx()
    C.nc = nc

    C.in_names = []

    def din(name, shape, dt=F32):
        C.in_names.append(name)
        return T(nc.dram_tensor(name, list(shape), dt, kind="ExternalInput").ap(), name)

    def dscr(name, shape, dt=F32):
        kind = "ExternalOutput" if name in debug_outs else "Internal"
        return T(nc.dram_tensor(name, list(shape), dt, kind=kind).ap(), name)

    x = din("x", [S, D])
    positions = din("positions", [S], I32)
    ln_in_g = din("ln_in_g", [D]); ln_in_b = din("ln_in_b", [D])
    w_in = din("w_in", [DEPTH, D, IN_COLS])
    C.mla_q_norm = din("mla_q_norm", [DEPTH, 768]); C.mla_kv_norm = din("mla_kv_norm", [DEPTH, 256])
    C.mla_w_uq = din("mla_w_uq", [DEPTH, 768, 768]); C.mla_w_ukv = din("mla_w_ukv", [DEPTH, 256, 1024])
    C.gdn_conv_w = din("gdn_conv_w", [DEPTH, 4 * 1536]); C.gdn_A_log = din("gdn_A_log", [DEPTH, 8])
    C.gdn_dt_bias = din("gdn_dt_bias", [DEPTH, 8]); C.gdn_o_norm = din("gdn_o_norm", [DEPTH, 64])
    C.gate_bias = din("gate_bias", [DEPTH, 3 * D]); C.w_branch = din("w_branch", [DEPTH, 3, 512, D])
    C.w_out = din("w_out", [DEPTH, D, D])
    C.ln1_g = din("ln1_g", [DEPTH, D]); C.ln1_b = din("ln1_b", [DEPTH, D])
    C.peer_w_q = din("peer_w_q", [DEPTH, D, 2048]); C.peer_skT = din("peer_skT", [DEPTH, 16, 128, 128])
    if stages is None or any(s_.startswith("peer") for s_ in stages):
        C.peer_u = din("peer_u", [DEPTH * 16384, D]); C.peer_v = din("peer_v", [DEPTH * 16384, D])
    C.ln2_g = din("ln2_g", [DEPTH, D]); C.ln2_b = din("ln2_b", [DEPTH, D])
    C.ident_d = din("ident", [128, 128])
    triu_d = din("triu", [128, 128])
    invf_d = din("invf", [128, 24])
    C.tri64_d = din("tri64", [64, 64]); C.trilS_d = din("trilS", [64, 64]); C.triuS_d = din("triuS", [64, 64])
    iota16_d = din("iota16", [128, 16])
    out = T(nc.dram_tensor("out", [S, D], F32, kind="ExternalOutput").ap(), "out")
    hbuf = dscr("hbuf", [S, D]); C.hbuf = hbuf
    C.proj = dscr("proj", [S, IN_COLS])
    C.qTa = dscr("qTa", [8, 96, S]); C.kTa = dscr("kTa", [8, 96, S]); C.va = dscr("va", [S, 8, 64]); C.oa = dscr("oa", [S, 512])
    C.qTb = dscr("qTb", [8, 64, S]); C.kTb = dscr("kTb", [8, 64, S]); C.vb = dscr("vb", [S, 8, 64]); C.ob = dscr("ob", [S, 512])
    C.biasTb = dscr("biasTb", [64, S])
    C.oc = dscr("oc", [S, 512])
    C.sc = dscr("sc", [S, 16, 128])

    S_ = Sched(nc, es); C.S_ = S_
    A = Arena(nc, es, 28000); C.A = A
    AR = Arena(nc, es, 25000, name="arena_r", rounded=True); C.AR = AR
    ps = [T(es.enter_context(nc.psum_tensor(f"ps{i}", [128, 512], F32))[:, :], f"ps{i}") for i in range(8)]
    C.ps = ps

    def want(name):
        return stages is None or name in stages

    ident = A.alloc(128, "ident"); S_.dma(ident, C.ident_d); C.ident = ident
    triu = A.alloc(128, "triu"); S_.dma(triu, triu_d); C.triu = triu
    C.iota16 = A.alloc(16, "iota16"); S_.dma(C.iota16, iota16_d)
    C.ones = A.alloc(64, "ones"); S_.memset(C.ones, 1.0)
    C.zeros = A.alloc(64, "zeros"); S_.memset(C.zeros, 0.0)
    rope = A.alloc(2 * NT * 24, "rope").re("p (k t j) -> p k t j", k=2, t=NT)
    sinT, cosT = rope[:, 0], rope[:, 1]
    C.cosA, C.sinA = cosT[:, :, 0:16], sinT[:, :, 0:16]
    C.cosB, C.sinB = cosT[:, :, 16:24], sinT[:, :, 16:24]
    base = A.mark()
    if want("init"):
        pos_i = A.alloc(NT, "pos_i").cast(I32)
        for t in range(NT):
            S_.dma(pos_i[:, t:t + 1], T(positions.ap[t * 128:(t + 1) * 128].rearrange("(p o) -> p o", o=1), "positions"))
        posf = A.alloc(NT, "posf"); invf = A.alloc(24, "invf")
        S_.dma(invf, invf_d)
        S_.copy(posf, pos_i)
        ang = A.alloc(NT * 24, "ang")
        S_.tt(ang.re("p (t j) -> p t j", j=24), posf.un(2).bc([128, NT, 24]), invf.un(1).bc([128, NT, 24]), ALU.mult)
        sin_range(S_, A, sinT.re("p t j -> p (t j)"), ang, 0.0)
        sin_range(S_, A, cosT.re("p t j -> p (t j)"), ang, math.pi / 2)
        S_.barrier()
        A.reset(base)

    if want("ln_in"):
        g_bc = A.alloc(D, "g_bc"); b_bc = A.alloc(D, "b_bc")
        S_.dma(g_bc, T(ln_in_g.ap.partition_broadcast(128), "ln_in_g"))
        S_.dma(b_bc, T(ln_in_b.ap.partition_broadcast(128), "ln_in_b"))
        xt = [A.alloc(D, f"xt{i}") for i in range(2)]
        tmp = A.alloc(D, "lntmp")
        st = [A.alloc(4, f"st{i}") for i in range(2)]
        for t in range(NT):
            b = t % 2
            S_.dma(xt[b], x[t * 128:(t + 1) * 128, :])
            layer_norm_tile(S_, xt[b], g_bc, b_bc, st[b], tmp)
            S_.dma(hbuf[t * 128:(t + 1) * 128, :].k(t), xt[b])
        S_.barrier()
        A.reset(base)

    for l in range(depth):
        if want(f"proj{l}"):
            hT = AR.alloc(8 * S, "hT").re("p (c t) -> p c t", c=8)
            ht = [A.alloc(D, f"ht{i}") for i in range(2)]
            for t in range(NT):
                b = t % 2
                S_.dma(ht[b], hbuf[t * 128:(t + 1) * 128, :].k(t))
                for c in range(8):
                    p = ps[c % 2]
                    S_.tr(p[:, 0:128], ht[b][:, c * 128:(c + 1) * 128], ident)
                    S_.copy(hT[:, c, t * 128:(t + 1) * 128].k((c, t)), p[:, 0:128], eng=("dve" if c % 2 == 0 else "act"))
            wt = [AR.alloc(8 * 512, f"wt{i}").re("p (c n) -> p c n", c=8) for i in range(2)]
            ot = [A.alloc(512, f"ot{i}") for i in range(4)]
            w_l = T(w_in.ap[l].rearrange("(c p) n -> p c n", p=128), "w_in")
            ncg = (IN_COLS + 511) // 512
            oi = 0
            for cg in range(ncg):
                c0 = cg * 512
                cw = min(512, IN_COLS - c0)
                wb = wt[cg % 2]
                S_.dma(wb[:, :, 0:cw], w_l[:, :, c0:c0 + cw].r)
                for t in range(NT):
                    p = ps[2 + (t % 4)]
                    for c in range(8):
                        S_.mm(p[:, 0:cw], hT[:, c, t * 128:(t + 1) * 128].k((c, t)), wb[:, c, 0:cw],
                              start=(c == 0), stop=(c == 7))
                    o = ot[oi % 4]
                    oi += 1
                    S_.copy(o[:, 0:cw], p[:, 0:cw], eng=("dve" if t % 2 == 0 else "act"))
                    S_.dma(C.proj[t * 128:(t + 1) * 128, c0:c0 + cw].k((t, cg)), o[:, 0:cw])
            S_.barrier()
            A.reset(base)
            AR.reset(0)
        if want(f"mla{l}"):
            mla_prep(C, l)
            attention(C, l, C.qTa, C.kTa, C.va, C.oa, 96, 96 ** -0.5)
        if want(f"moba{l}"):
            moba_prep(C, l)
            attention(C, l, C.qTb, C.kTb, C.vb, C.ob, 64, 0.125, biasT_d=C.biasTb)
        if want(f"gdn{l}"):
            gdn_stage(C, l)
        if want(f"merge{l}"):
            merge_stage(C, l)
        if want(f"peer{l}"):
            peer_stage(C, l)

    if want("final"):
        ft = [A.alloc(D, f"ft{i}") for i in range(2)]
        for t in range(NT):
            S_.dma(ft[t % 2], hbuf[t * 128:(t + 1) * 128, :].k(t))
            S_.dma(out[t * 128:(t + 1) * 128, :], ft[t % 2])
        S_.barrier()

    S_.emit()
    nc._keep = (es, S_, A, AR)
    nc._in_names = list(C.in_names)
    return nc


def make_in_maps(inputs):
    n = 8
    f32 = np.float32
    i128 = np.arange(128)
    i64 = np.arange(64)
    invfA = 500000.0 ** (-np.arange(0, 32, 2, dtype=f32) / f32(32))
    invfB = 500000.0 ** (-np.arange(0, 16, 2, dtype=f32) / f32(16))
    consts = {
        "ident": np.eye(128, dtype=f32),
        "triu": (i128[None, :] >= i128[:, None]).astype(f32),
        "invf": np.tile(np.concatenate([invfA, invfB]).astype(f32)[None, :], (128, 1)),
        "tri64": (i64[:, None] <= i64[None, :]).astype(f32),
        "trilS": (i64[None, :] < i64[:, None]).astype(f32),
        "triuS": (i64[None, :] > i64[:, None]).astype(f32),
        "iota16": np.tile(np.arange(16, dtype=f32)[None, :], (128, 1)),
    }
    shared = {}
    for k in ("ln_in_g", "ln_in_b", "w_in", "mla_q_norm", "mla_kv_norm", "mla_w_uq", "mla_w_ukv", "gdn_A_log",
              "gdn_dt_bias", "gdn_o_norm", "w_branch", "w_out", "ln1_g", "ln1_b", "peer_w_q", "ln2_g", "ln2_b"):
        shared[k] = np.ascontiguousarray(inputs[k], dtype=f32)
    shared["gdn_conv_w"] = np.ascontiguousarray(inputs["gdn_conv_w"], dtype=f32).reshape(DEPTH, 4 * 1536)
    shared["gate_bias"] = np.ascontiguousarray(inputs["gate_bias"], dtype=f32).reshape(DEPTH, 3 * D)
    sk = np.asarray(inputs["peer_sub_keys"], dtype=f32)
    shared["peer_skT"] = np.ascontiguousarray(sk.transpose(0, 1, 2, 4, 3).reshape(DEPTH, 16, 128, 128))
    shared["peer_u"] = np.ascontiguousarray(inputs["peer_u"], dtype=f32).reshape(DEPTH * 16384, D)
    shared["peer_v"] = np.ascontiguousarray(inputs["peer_v"], dtype=f32).reshape(DEPTH * 16384, D)
    maps = []
    for b in range(n):
        m = dict(shared)
        m.update(consts)
        m["x"] = np.ascontiguousarray(inputs["x"][b], dtype=f32)
        m["positions"] = np.ascontiguousarray(inputs["positions"][b], dtype=np.int32)
        maps.append(m)
    return maps


def kernel(**inputs):
    inputs = {k: np.asarray(v) for k, v in inputs.items()}
    nc = build()
    res = run_bass_kernel_spmd(nc, make_in_maps(inputs), core_ids=list(range(8)))
    return np.stack([np.asarray(r["out"], dtype=np.float32) for r in res.results], axis=0)
```
